# Optimizing a Trainium2 kernel written in Bass

```python
import jax, jax.numpy as jnp
from jax import lax
import numpy as np

D_MODEL = 1024
BATCH = 8
SEQ = 4096
DEPTH = 4

CHUNK = 64
Q_BLOCK = 128
EPS = 1e-6
NEG_INF = -1e30

MLA_HEADS = 8
MLA_NOPE = 64
MLA_ROPE = 32
MLA_QK = MLA_NOPE + MLA_ROPE
MLA_V = 64
Q_LORA = 256
KV_LORA = 128
ROPE_BASE = 10000.0
MLA_WIDTH = MLA_HEADS * MLA_V

CA_HEADS = 8
CA_HEAD_DIM = 64
CA_WIDTH = CA_HEADS * CA_HEAD_DIM
LEFT_CHUNKS = 8
BAND_CHUNKS = LEFT_CHUNKS + 1
BAND = BAND_CHUNKS * CHUNK
REL_CLIP = 128
N_REL = 2 * REL_CLIP + 1

D_MIX = MLA_WIDTH + CA_WIDTH
OFF_CQ = 0
OFF_CKV = OFF_CQ + Q_LORA
OFF_KR = OFF_CKV + KV_LORA
OFF_CA = OFF_KR + MLA_ROPE
IN_COLS = OFF_CA + 3 * CA_WIDTH

D_FF = 2816
CONV_W = 3

kernel_name = "hybrid_mla_chunkrel_convglu"


def rmsnorm(x, g):
    xf = x.astype(jnp.float32)
    y = xf * lax.rsqrt(jnp.mean(xf * xf, axis=-1, keepdims=True) + EPS)
    return (y * g.astype(jnp.float32)).astype(x.dtype)


def rope_tables(positions):
    inv = ROPE_BASE ** (-jnp.arange(0, MLA_ROPE, 2, dtype=jnp.float32) / MLA_ROPE)
    ang = positions.astype(jnp.float32)[..., None] * inv
    return jnp.cos(ang), jnp.sin(ang)


def apply_rope(t, cos, sin):
    t1, t2 = jnp.split(t.astype(jnp.float32), 2, axis=-1)
    c = cos[:, :, None, :]
    s = sin[:, :, None, :]
    return jnp.concatenate([t1 * c - t2 * s, t1 * s + t2 * c], axis=-1).astype(t.dtype)


def rope_tail(t, cos, sin):
    return jnp.concatenate([t[..., :MLA_NOPE], apply_rope(t[..., MLA_NOPE:], cos, sin)], axis=-1)


def mla_mixer(c_q, c_kv, k_rope, cos, sin, w_uq, w_ukv, g_qa, g_kva, g_qn, g_kn):
    B, S, _ = c_q.shape
    q = (rmsnorm(c_q, g_qa) @ w_uq).reshape(B, S, MLA_HEADS, MLA_QK)
    kv = (rmsnorm(c_kv, g_kva) @ w_ukv).reshape(B, S, MLA_HEADS, MLA_NOPE + MLA_V)
    k_nope, v = kv[..., :MLA_NOPE], kv[..., MLA_NOPE:]
    k_r = jnp.broadcast_to(k_rope[:, :, None, :], (B, S, MLA_HEADS, MLA_ROPE))
    k = jnp.concatenate([k_nope, k_r], axis=-1)
    q = rope_tail(rmsnorm(q, g_qn), cos, sin)
    k = rope_tail(rmsnorm(k, g_kn), cos, sin)
    scale = MLA_QK ** -0.5
    n_qb = S // Q_BLOCK
    q_blocks = q.reshape(B, n_qb, Q_BLOCK, MLA_HEADS, MLA_QK).transpose(1, 0, 2, 3, 4)
    k_chunk = jnp.arange(S) // CHUNK

    def one_block(args):
        qb, bidx = args
        q_chunk = (bidx * Q_BLOCK + jnp.arange(Q_BLOCK)) // CHUNK
        s = jnp.einsum('bqhd,bkhd->bhqk', qb, k).astype(jnp.float32) * scale
        mask = k_chunk[None, :] <= q_chunk[:, None]
        s = jnp.where(mask[None, None], s, NEG_INF)
        p = jax.nn.softmax(s, axis=-1).astype(v.dtype)
        return jnp.einsum('bhqk,bkhd->bqhd', p, v)

    out = lax.map(one_block, (q_blocks, jnp.arange(n_qb)))
    return out.transpose(1, 0, 2, 3, 4).reshape(B, S, MLA_WIDTH)


def chunk_rel_mixer(qkv, rel_bias, g_qn, g_kn):
    B, S, _ = qkv.shape
    nc = S // CHUNK
    qkv = qkv.reshape(B, S, 3, CA_HEADS, CA_HEAD_DIM)
    q = rmsnorm(qkv[:, :, 0], g_qn)
    k = rmsnorm(qkv[:, :, 1], g_kn)
    v = qkv[:, :, 2]
    qc = q.reshape(B, nc, CHUNK, CA_HEADS, CA_HEAD_DIM)

    def band(t):
        tc = t.reshape(B, nc, CHUNK, CA_HEADS, CA_HEAD_DIM)
        tp = jnp.pad(tc, ((0, 0), (LEFT_CHUNKS, 0), (0, 0), (0, 0), (0, 0)))
        return jnp.concatenate([tp[:, j:j + nc] for j in range(BAND_CHUNKS)], axis=2)

    kb, vb = band(k), band(v)
    scale = CA_HEAD_DIM ** -0.5
    s = jnp.einsum('bcqhd,bckhd->bhcqk', qc, kb).astype(jnp.float32) * scale
    qi = jnp.arange(CHUNK)
    kj = jnp.arange(BAND)
    dist = qi[:, None] + LEFT_CHUNKS * CHUNK - kj[None, :]
    idx = jnp.clip(dist, -REL_CLIP, REL_CLIP) + REL_CLIP
    bias = rel_bias[:, idx].astype(jnp.float32)
    valid = (jnp.arange(nc)[:, None] - LEFT_CHUNKS + kj[None, :] // CHUNK) >= 0
    s = jnp.where(valid[None, None, :, None, :], s + bias[None, :, None], NEG_INF)
    p = jax.nn.softmax(s, axis=-1).astype(vb.dtype)
    o = jnp.einsum('bhcqk,bckhd->bcqhd', p, vb)
    return o.reshape(B, S, CA_WIDTH)


def conv_glu_ffn(h, w_up, conv_w, conv_b, w_down):
    S = h.shape[1]
    u = h @ w_up
    up = jnp.pad(u, ((0, 0), (CONV_W - 1, 0), (0, 0)))
    u = sum(up[:, i:i + S] * conv_w[i] for i in range(CONV_W)) + conv_b
    g, val = u[..., :D_FF], u[..., D_FF:]
    return (jax.nn.silu(g) * val) @ w_down


def setup_inputs(seed: int = 0) -> dict:
    key = jax.random.key(seed)
    ks = jax.random.split(key, 24)
    f32 = jnp.float32

    def nrm(k, shape, scale):
        return jax.random.normal(k, shape, f32) * scale

    def gain(k, shape):
        return 1.0 + 0.02 * jax.random.normal(k, shape, f32)

    x = jax.random.normal(ks[0], (BATCH, SEQ, D_MODEL), f32)
    positions = jnp.broadcast_to(jnp.arange(SEQ, dtype=jnp.int32)[None, :], (BATCH, SEQ))
    return {
        "x": x,
        "positions": positions,
        "g_mix": gain(ks[1], (DEPTH, D_MODEL)),
        "w_in": nrm(ks[2], (DEPTH, D_MODEL, IN_COLS), D_MODEL ** -0.5),
        "w_uq": nrm(ks[3], (DEPTH, Q_LORA, MLA_HEADS * MLA_QK), Q_LORA ** -0.5),
        "w_ukv": nrm(ks[4], (DEPTH, KV_LORA, MLA_HEADS * (MLA_NOPE + MLA_V)), KV_LORA ** -0.5),
        "g_q_lora": gain(ks[5], (DEPTH, Q_LORA)),
        "g_kv_lora": gain(ks[6], (DEPTH, KV_LORA)),
        "g_mla_q": gain(ks[7], (DEPTH, MLA_QK)),
        "g_mla_k": gain(ks[8], (DEPTH, MLA_QK)),
        "g_ca_q": gain(ks[9], (DEPTH, CA_HEAD_DIM)),
        "g_ca_k": gain(ks[10], (DEPTH, CA_HEAD_DIM)),
        "rel_bias": nrm(ks[11], (DEPTH, CA_HEADS, N_REL), 0.2),
        "g_out_mla": gain(ks[12], (DEPTH, MLA_WIDTH)),
        "g_out_ca": gain(ks[13], (DEPTH, CA_WIDTH)),
        "w_out": nrm(ks[14], (DEPTH, D_MIX, D_MODEL), D_MIX ** -0.5),
        "g_ffn": gain(ks[15], (DEPTH, D_MODEL)),
        "w_up": nrm(ks[16], (DEPTH, D_MODEL, 2 * D_FF), D_MODEL ** -0.5),
        "conv_w": nrm(ks[17], (DEPTH, CONV_W, 2 * D_FF), CONV_W ** -0.5),
        "conv_b": nrm(ks[18], (DEPTH, 2 * D_FF), 0.02),
        "w_down": nrm(ks[19], (DEPTH, D_FF, D_MODEL), D_FF ** -0.5),
    }


def reference(x, positions, g_mix, w_in, w_uq, w_ukv, g_q_lora, g_kv_lora, g_mla_q, g_mla_k,
              g_ca_q, g_ca_k, rel_bias, g_out_mla, g_out_ca, w_out, g_ffn, w_up, conv_w,
              conv_b, w_down):
    cos, sin = rope_tables(positions)
    for l in range(DEPTH):
        h = rmsnorm(x, g_mix[l])
        proj = h @ w_in[l]
        c_q = proj[..., OFF_CQ:OFF_CKV]
        c_kv = proj[..., OFF_CKV:OFF_KR]
        k_rope = proj[..., OFF_KR:OFF_CA]
        qkv_b = proj[..., OFF_CA:]
        o_a = mla_mixer(c_q, c_kv, k_rope, cos, sin, w_uq[l], w_ukv[l], g_q_lora[l],
                        g_kv_lora[l], g_mla_q[l], g_mla_k[l])
        o_b = chunk_rel_mixer(qkv_b, rel_bias[l], g_ca_q[l], g_ca_k[l])
        o = jnp.concatenate([rmsnorm(o_a, g_out_mla[l]), rmsnorm(o_b, g_out_ca[l])], axis=-1)
        x = x + o @ w_out[l]
        x = x + conv_glu_ffn(rmsnorm(x, g_ffn[l]), w_up[l], conv_w[l], conv_b[l], w_down[l])
    return x
```

```python
import numpy as np
from contextlib import ExitStack
import concourse.bass as bass
import concourse.mybir as mybir
from concourse.bass_utils import run_bass_kernel_spmd

F32 = mybir.dt.float32
BF16 = mybir.dt.bfloat16
I32 = mybir.dt.int32
AF = mybir.ActivationFunctionType
ALU = mybir.AluOpType
AX = mybir.AxisListType

D = 1024
SEQ = 4096
DEPTH = 4
NT = SEQ // 128
NST = NT // 4
EPS = 1e-6
H = 8
QK = 96
INC = 1952
DFF = 2816
NCH = DFF // 128
EPOCH = 30000
NRING = 6
VW = 72


class Res:
    __slots__ = ("w", "r")

    def __init__(self):
        self.w = None
        self.r = {}


class Buf:
    def __init__(self, t):
        self.t = t
        self.r = Res()


class Eng:
    def __init__(self, S, name, obj):
        self.S = S
        self.name = name
        self.obj = obj
        self.count = 0
        self.sems = []
        self.seen = {}

    def sem_for(self, epoch):
        while len(self.sems) <= epoch:
            self.sems.append(self.S.new_sem(f"{self.name}_e{len(self.sems)}"))
        return self.sems[epoch]


class Sched:
    def __init__(self, nc, stack):
        self.nc = nc
        self.stack = stack
        self.engs = {
            "pe": Eng(self, "pe", nc.tensor),
            "act": Eng(self, "act", nc.scalar),
            "dve": Eng(self, "dve", nc.vector),
            "pool": Eng(self, "pool", nc.gpsimd),
            "sp": Eng(self, "sp", nc.sync),
        }
        self.dma_rings = {}
        self.keysem = {}
        self.nwait = 0
        self.nops = 0
        self.limit = None

    def new_sem(self, name):
        return self.stack.enter_context(self.nc.semaphore(name))

    def _wait(self, eng, ev):
        key, val = ev
        e = self.engs[eng]
        if e.seen.get(key, 0) >= val:
            return
        e.obj.wait_ge(self.keysem[key], val)
        e.seen[key] = val
        self.nwait += 1

    def _deps(self, eng, reads, writes):
        e = self.engs[eng]
        best = {}

        def add(k, v, raw):
            if k[0] == eng:
                if not raw and eng == "pe":
                    return
                if v + k[1] * EPOCH > e.count:
                    return
            if best.get(k, 0) < v:
                best[k] = v

        for r in reads:
            if r.w is not None:
                add(r.w[0], r.w[1], True)
        for w in writes:
            if w.w is not None:
                add(w.w[0], w.w[1], False)
            for k, v in w.r.items():
                add(k, v, False)
        for k, v in best.items():
            self._wait(eng, (k, v))

    def op(self, eng, fn, reads=(), writes=(), inc=True):
        self.nops += 1
        if self.limit is not None and self.nops > self.limit:
            return None
        e = self.engs[eng]
        reads = [b.r if isinstance(b, Buf) else b for b in reads]
        writes = [b.r if isinstance(b, Buf) else b for b in writes]
        self._deps(eng, reads, writes)
        n = e.count + 1
        epoch = (n - 1) // EPOCH
        sem = e.sem_for(epoch)
        key = (eng, epoch)
        self.keysem[key] = sem
        val = n - epoch * EPOCH
        inst = fn()
        if inc:
            inst.then_inc(sem, 1)
            e.count += 1
        for r in reads:
            if r.r.get(key, 0) < val:
                r.r[key] = val
        for w in writes:
            w.w = (key, val)
            w.r = {}
        return inst

    def dma(self, q, out, in_, reads=(), writes=(), **kw):
        self.nops += 1
        if self.limit is not None and self.nops > self.limit:
            return None
        e = self.engs[q]
        reads = [b.r if isinstance(b, Buf) else b for b in reads]
        writes = [b.r if isinstance(b, Buf) else b for b in writes]
        ring = self.dma_rings.setdefault(q, {"sems": [], "vals": [], "i": 0})
        NR = 12
        if len(ring["sems"]) < NR:
            ring["sems"].append(self.new_sem(f"dma_{q}_{len(ring['sems'])}"))
            ring["vals"].append(0)
            idx = len(ring["sems"]) - 1
        else:
            idx = ring["i"] % NR
        ring["i"] += 1
        sem = ring["sems"][idx]
        key = ("dma", q, idx)
        self.keysem[key] = sem
        if ring["vals"][idx] > 0:
            self._wait(q, (key, ring["vals"][idx]))
        self._deps(q, reads, writes)
        inst = e.obj.dma_start(out=out, in_=in_, **kw)
        inst.then_inc(sem, 16)
        ring["vals"][idx] += 16
        val = ring["vals"][idx]
        for r in reads:
            if r.r.get(key, 0) < val:
                r.r[key] = val
        for w in writes:
            w.w = (key, val)
            w.r = {}
        return inst

    def finish(self, res_list):
        for r in res_list:
            r = r.r if isinstance(r, Buf) else r
            if r.w is not None:
                self._wait("sp", r.w)

    def barrier_all(self):
        evs = []
        for name, e in self.engs.items():
            if e.count > 0:
                n = e.count
                epoch = (n - 1) // EPOCH
                evs.append(((name, epoch), n - epoch * EPOCH))
        for q, ring in self.dma_rings.items():
            for idx, v in enumerate(ring["vals"]):
                if v > 0:
                    evs.append((("dma", q, idx), v))
        for name in self.engs:
            for ev in evs:
                self._wait(name, ev)


def build(NL=DEPTH, dbg=None, limit=None):
    nc = bass.Bass("TRN2", target_bir_lowering=False)
    dt_in = lambda name, shape, dt=F32: nc.dram_tensor(name, list(shape), dt, kind="ExternalInput").ap()
    x_d = dt_in("x", [SEQ, D])
    pos_d = dt_in("positions", [SEQ], I32)
    g_mix_d = dt_in("g_mix", [DEPTH, D])
    w_in_d = dt_in("w_in", [DEPTH, D, INC])
    w_uq_d = dt_in("w_uq", [DEPTH, 256, 768])
    w_ukv_d = dt_in("w_ukv", [DEPTH, 128, 1024])
    g_ql_d = dt_in("g_q_lora", [DEPTH, 256])
    g_kvl_d = dt_in("g_kv_lora", [DEPTH, 128])
    g_mq_d = dt_in("g_mla_q", [DEPTH, 96])
    g_mk_d = dt_in("g_mla_k", [DEPTH, 96])
    g_cq_d = dt_in("g_ca_q", [DEPTH, 64])
    g_ck_d = dt_in("g_ca_k", [DEPTH, 64])
    relb_d = dt_in("rel_bias", [DEPTH, H, 257])
    g_om_d = dt_in("g_out_mla", [DEPTH, 512])
    g_oc_d = dt_in("g_out_ca", [DEPTH, 512])
    w_out_d = dt_in("w_out", [DEPTH, D, D])
    g_ffn_d = dt_in("g_ffn", [DEPTH, D])
    w_up_l = [dt_in(f"w_up_l{i}", [D, 2 * DFF]) for i in range(DEPTH)]
    conv_w_d = dt_in("conv_w", [DEPTH, 3, 2 * DFF])
    conv_b_d = dt_in("conv_b", [DEPTH, 2 * DFF])
    w_down_l = [dt_in(f"w_down_l{i}", [DFF, D]) for i in range(DEPTH)]
    y_d = nc.dram_tensor("y", [SEQ, D], F32, kind="ExternalOutput").ap()
    kT_d = nc.dram_tensor("kT_s", [H, 128, SEQ], BF16, kind="ExternalOutput").ap()
    v_d = nc.dram_tensor("v_s", [H, 128, NT, 64], BF16, kind="ExternalOutput").ap()
    ext_d = nc.dram_tensor("ext_s", [H, 768], F32, kind="Internal").ap()
    wup_s = nc.dram_tensor("wup_s", [D, 2 * DFF], BF16, kind="ExternalOutput").ap()
    wdn_s = nc.dram_tensor("wdn_s", [DFF, D], BF16, kind="ExternalOutput").ap()

    with ExitStack() as top:
        S = Sched(nc, top)
        S.limit = limit

        uniq = [0]

        def sb(st, name, shape, dt=F32):
            uniq[0] += 1
            return Buf(st.enter_context(nc.sbuf_tensor(f"{name}_u{uniq[0]}", list(shape), dt)))

        V = lambda fn, r=(), w=(): S.op("dve", fn, r, w)
        A = lambda fn, r=(), w=(): S.op("act", fn, r, w)
        G = lambda fn, r=(), w=(): S.op("pool", fn, r, w)
        PE = lambda fn, r=(), w=(), inc=True: S.op("pe", fn, r, w, inc)
        DMA = lambda out, in_, r=(), w=(), **kw: S.dma("sp", out, in_, r, w, **kw)

        banks = [Buf(top.enter_context(nc.psum_tensor(f"bank{i}", [128, 512], F32))) for i in range(8)]

        def bf(bank):
            return bank.t[:].bitcast(BF16)

        ident_b = sb(top, "ident_b", [128, 128], BF16)
        ident_f = sb(top, "ident_f", [128, 128], F32)
        Jf = sb(top, "Jf", [128, 128], F32)
        ones_f = sb(top, "ones_f", [128, 128], F32)
        eps_t = sb(top, "eps_t", [128, 1], F32)
        maskb = sb(top, "maskb", [128, 1], F32)
        cos_t = sb(top, "cos_t", [128, NT, 16], F32)
        sin_t = sb(top, "sin_t", [128, NT, 16], F32)
        r_y = [Res() for _ in range(NT)]
        r_kv = [Res() for _ in range(NST)]
        r_ext = Res()
        r_wup = Res()
        r_wdn = Res()

        G(lambda: nc.gpsimd.memset(ones_f.t[:], 1.0), [], [ones_f])
        G(lambda: nc.gpsimd.affine_select(out=ident_f.t[:], in_=ones_f.t[:], pattern=[[1, 128]],
                                          compare_op=ALU.is_equal, fill=0.0, base=0, channel_multiplier=-1),
          [ones_f], [ident_f])
        G(lambda: nc.gpsimd.affine_select(out=Jf.t[:], in_=ones_f.t[:], pattern=[[1, 128]],
                                          compare_op=ALU.is_equal, fill=0.0, base=-127, channel_multiplier=1),
          [ones_f], [Jf])
        G(lambda: nc.gpsimd.tensor_copy(out=ident_b.t[:], in_=ident_f.t[:]), [ident_f], [ident_b])
        G(lambda: nc.gpsimd.memset(eps_t.t[:], EPS), [], [eps_t])
        G(lambda: nc.gpsimd.memset(maskb.t[:], 0.0), [], [maskb])
        G(lambda: nc.gpsimd.memset(maskb.t[64:128, :], -30000.0), [], [maskb])

        with ExitStack() as st:
            pos_i = sb(st, "pos_i", [NT, 128], I32)
            pos_f = sb(st, "pos_f", [NT, 128], F32)
            posT = sb(st, "posT", [128, NT], F32)
            invf = sb(st, "invf", [128, 16], F32)
            ang = sb(st, "ang", [128, NT, 16], F32)
            ki = sb(st, "ki", [128, NT, 16], I32)
            kf = sb(st, "kf", [128, NT, 16], F32)
            rr = sb(st, "rr", [128, NT, 16], F32)
            DMA(pos_i.t[:], pos_d.rearrange("(t p) -> t p", p=128), [], [pos_i])
            V(lambda: nc.vector.tensor_copy(out=pos_f.t[:], in_=pos_i.t[:]), [pos_i], [pos_f])
            PE(lambda: nc.tensor.transpose(out=banks[0].t[:, 0:NT], in_=pos_f.t[:], identity=ident_f.t[0:NT, 0:NT]),
               [pos_f, ident_f], [banks[0]])
            V(lambda: nc.vector.tensor_copy(out=posT.t[:], in_=banks[0].t[:, 0:NT]), [banks[0]], [posT])
            for j in range(16):
                G(lambda: nc.gpsimd.memset(invf.t[:, j:j + 1], float(np.float32(10000.0) ** np.float32(-(2.0 * j) / 32.0))),
                  [], [invf])
            V(lambda: nc.vector.tensor_tensor(out=ang.t[:], in0=posT.t[:, :].unsqueeze(2).broadcast_to([128, NT, 16]),
                                              in1=invf.t[:, :].unsqueeze(1).broadcast_to([128, NT, 16]), op=ALU.mult),
              [posT, invf], [ang])
            for (shift, dst) in ((0.0, sin_t), (float(np.pi / 2), cos_t)):
                V(lambda: nc.vector.tensor_scalar(out=ki.t[:], in0=ang.t[:], scalar1=shift, scalar2=float(1.0 / (2 * np.pi)),
                                                  op0=ALU.add, op1=ALU.mult), [ang], [ki])
                V(lambda: nc.vector.tensor_copy(out=kf.t[:], in_=ki.t[:]), [ki], [kf])
                V(lambda: nc.vector.scalar_tensor_tensor(out=rr.t[:], in0=kf.t[:], scalar=-6.28125, in1=ang.t[:],
                                                         op0=ALU.mult, op1=ALU.add), [kf, ang], [rr])
                V(lambda: nc.vector.tensor_scalar(out=rr.t[:], in0=rr.t[:], scalar1=shift, scalar2=None, op0=ALU.add),
                  [rr], [rr])
                V(lambda: nc.vector.scalar_tensor_tensor(out=rr.t[:], in0=kf.t[:], scalar=-0.0019353071795864769,
                                                         in1=rr.t[:], op0=ALU.mult, op1=ALU.add), [kf, rr], [rr])
                V(lambda: nc.vector.tensor_scalar(out=kf.t[:], in0=rr.t[:], scalar1=float(np.pi), scalar2=float(-2 * np.pi),
                                                  op0=ALU.is_gt, op1=ALU.mult), [rr], [kf])
                V(lambda: nc.vector.tensor_tensor(out=rr.t[:], in0=rr.t[:], in1=kf.t[:], op=ALU.add), [rr, kf], [rr])
                V(lambda: nc.vector.tensor_scalar(out=kf.t[:], in0=rr.t[:], scalar1=float(-np.pi), scalar2=float(2 * np.pi),
                                                  op0=ALU.is_lt, op1=ALU.mult), [rr], [kf])
                V(lambda: nc.vector.tensor_tensor(out=rr.t[:], in0=rr.t[:], in1=kf.t[:], op=ALU.add), [rr, kf], [rr])
                A(lambda: nc.scalar.activation(out=dst.t[:], in_=rr.t[:], func=AF.Sin), [rr], [dst])
            S.barrier_all()
        if dbg == "s1":
            return nc

        def rstd_of(ms_ap, out_ap, scale, r, w, tmp):
            n = ms_ap.shape[-1]
            A(lambda: nc.scalar.activation(out=tmp.t[:, 0:n], in_=ms_ap, func=AF.Ln, bias=eps_t.t[:, 0:1], scale=scale),
              list(r) + [eps_t], [tmp])
            A(lambda: nc.scalar.activation(out=out_ap, in_=tmp.t[:, 0:n], func=AF.Exp, scale=-0.5), [tmp], list(w))

        for l in range(NL):
            x_src = x_d if l == 0 else y_d
            r_xsrc = [Res() for _ in range(NT)] if l == 0 else r_y

            with ExitStack() as st:
                Win = sb(st, "Win", [128, 8, INC], BF16)
                Wuq = sb(st, "Wuq", [128, 2, 768], BF16)
                Wukv = sb(st, "Wukv", [128, 1024], BF16)
                Wout = sb(st, "Wout", [128, 8, 1024], BF16)
                wst = [sb(st, f"wst{i}", [128, 1024], F32) for i in range(2)]
                wcb = [sb(st, f"wcb{i}", [128, 1024], BF16) for i in range(2)]
                gpk = sb(st, "gpk", [128, 32], F32)
                gq_b = sb(st, "gq_b", [128, 96], F32)
                gk_b = sb(st, "gk_b", [128, 96], F32)
                gcq_b = sb(st, "gcq_b", [128, 64], F32)
                gck_b = sb(st, "gck_b", [128, 64], F32)
                Et = sb(st, "Et", [128, H, 5, 128], BF16)
                xt = [sb(st, f"xt{i}", [128, D], F32) for i in range(2)]
                xr = [sb(st, "xr0", [128, D], F32)]
                junk = sb(st, "junk", [128, D], BF16)
                xb = sb(st, "xb", [128, D], BF16)
                xT = sb(st, "xT", [128, 8, 128], BF16)
                proj = sb(st, "proj", [128, INC], F32)
                sq = sb(st, "sq", [128, 1024], F32)
                stat = sb(st, "stat", [128, 64], F32)
                stmp = sb(st, "stmp", [128, 64], F32)
                rs = sb(st, "rs", [128, 64], F32)
                cb16 = sb(st, "cb16", [128, 384], BF16)
                cT = sb(st, "cT", [128, 3, 128], BF16)
                q_sb = sb(st, "q_sb", [128, H, 96], F32)
                kv_sb = sb(st, "kv_sb", [128, H, 128], F32)
                qrope = sb(st, "qrope", [128, H, 32], F32)
                rtmp = sb(st, "rtmp", [128, 4, H, 16], F32)
                krg = sb(st, "krg", [128, 32], F32)
                krr = sb(st, "krr", [128, 32], F32)
                ktmp = sb(st, "ktmp", [128, 4, 16], F32)
                qn = sb(st, "qn", [128, H, 128], BF16)
                kn = sb(st, "kn", [128, H, 128], BF16)
                vb = sb(st, "vb", [128, H, 64], BF16)
                caqn = sb(st, "caqn", [128, H, 64], BF16)
                cakn = sb(st, "cakn", [128, H, 64], BF16)
                qT_st = sb(st, "qT_st", [128, H, 512], BF16)
                kT_stg = [sb(st, f"kT_stg{i}", [128, H, 128], BF16) for i in range(2)]
                caqT = sb(st, "caqT", [128, 4, 128], BF16)
                cakT = [sb(st, f"cakT{i}", [128, 4, 128], BF16) for i in range(NRING)]
                caV = [sb(st, f"caV{i}", [128, H, VW], BF16) for i in range(NRING)]
                caP = [sb(st, f"caP{i}", [128, 5, 128], BF16) for i in range(2)]
                caPe = [sb(st, f"caPe{i}", [128, 5, 128], BF16) for i in range(2)]
                kTh = [sb(st, f"kTh{i}", [128, SEQ], BF16) for i in range(2)]
                Vh = [sb(st, f"Vh{i}", [128, NT, VW], BF16) for i in range(2)]
                PT = [sb(st, f"PT{i}", [128, 512], BF16) for i in range(4)]
                og = xb
                oT = xT

                DMA(gpk.t[:, 0:8], g_mix_d[l].rearrange("(k p) -> p k", p=128), [], [gpk], allow_slow_non_contiguous=True)
                DMA(gpk.t[:, 8:16], g_ffn_d[l].rearrange("(k p) -> p k", p=128), [], [gpk], allow_slow_non_contiguous=True)
                DMA(gpk.t[:, 16:20], g_om_d[l].rearrange("(k p) -> p k", p=128), [], [gpk], allow_slow_non_contiguous=True)
                DMA(gpk.t[:, 20:24], g_oc_d[l].rearrange("(k p) -> p k", p=128), [], [gpk], allow_slow_non_contiguous=True)
                DMA(gpk.t[:, 24:26], g_ql_d[l].rearrange("(k p) -> p k", p=128), [], [gpk], allow_slow_non_contiguous=True)
                DMA(gpk.t[:, 26:27], g_kvl_d[l].rearrange("(k p) -> p k", p=128), [], [gpk], allow_slow_non_contiguous=True)
                DMA(gq_b.t[:], g_mq_d[l].partition_broadcast(128), [], [gq_b])
                DMA(gk_b.t[:], g_mk_d[l].partition_broadcast(128), [], [gk_b])
                DMA(gcq_b.t[:], g_cq_d[l].partition_broadcast(128), [], [gcq_b])
                DMA(gck_b.t[:], g_ck_d[l].partition_broadcast(128), [], [gck_b])
                G(lambda: nc.gpsimd.tensor_scalar(out=gq_b.t[:], in0=gq_b.t[:], scalar1=float(QK ** -0.5), scalar2=1.0,
                                                  op0=ALU.mult, op1=ALU.mult), [gq_b], [gq_b])
                G(lambda: nc.gpsimd.tensor_scalar(out=gcq_b.t[:], in0=gcq_b.t[:], scalar1=0.125, scalar2=1.0,
                                                  op0=ALU.mult, op1=ALU.mult), [gcq_b], [gcq_b])

                wcnt = [0]

                def load_w(dst_ap, src_ap, gcol, dst_buf, ncols):
                    i = wcnt[0] % 2
                    wcnt[0] += 1
                    DMA(wst[i].t[:, 0:ncols], src_ap, [], [wst[i]])
                    G(lambda: nc.gpsimd.tensor_scalar(out=dst_ap, in0=wst[i].t[:, 0:ncols], scalar1=gpk.t[:, gcol:gcol + 1],
                                                      scalar2=1.0, op0=ALU.mult, op1=ALU.mult),
                      [wst[i], gpk], [dst_buf])

                for k in range(8):
                    for hlf in range(2):
                        c0 = hlf * 976
                        load_w(Win.t[:, k, c0:c0 + 976], w_in_d[l, k * 128:(k + 1) * 128, c0:c0 + 976], k, Win, 976)
                for k in range(2):
                    load_w(Wuq.t[:, k, :], w_uq_d[l, k * 128:(k + 1) * 128, :], 24 + k, Wuq, 768)
                load_w(Wukv.t[:, :], w_ukv_d[l, :, :], 26, Wukv, 1024)
                for k in range(8):
                    load_w(Wout.t[:, k, :], w_out_d[l, k * 128:(k + 1) * 128, :], 16 + k, Wout, 1024)

                ffn_pieces = []
                for k in range(8):
                    for c0 in range(0, 2 * DFF, 1024):
                        n = min(1024, 2 * DFF - c0)
                        ffn_pieces.append(("up", k, c0, n))
                for c in range(NCH):
                    ffn_pieces.append(("dn", c, 0, 1024))
                ffn_pieces.reverse()

                def convert_piece():
                    if not ffn_pieces:
                        return
                    kind, k, c0, n = ffn_pieces.pop()
                    i = wcnt[0] % 2
                    wcnt[0] += 1
                    if kind == "up":
                        DMA(wst[i].t[:, 0:n], w_up_l[l][k * 128:(k + 1) * 128, c0:c0 + n], [], [wst[i]])
                        G(lambda: nc.gpsimd.tensor_scalar(out=wcb[i].t[:, 0:n], in0=wst[i].t[:, 0:n],
                                                          scalar1=gpk.t[:, 8 + k:9 + k], scalar2=1.0,
                                                          op0=ALU.mult, op1=ALU.mult), [wst[i], gpk], [wcb[i]])
                        DMA(wup_s[k * 128:(k + 1) * 128, c0:c0 + n], wcb[i].t[:, 0:n], [wcb[i]], [r_wup])
                    else:
                        DMA(wst[i].t[:, 0:n], w_down_l[l][k * 128:(k + 1) * 128, :], [], [wst[i]])
                        G(lambda: nc.gpsimd.tensor_copy(out=wcb[i].t[:, 0:n], in_=wst[i].t[:, 0:n]), [wst[i]], [wcb[i]])
                        DMA(wdn_s[k * 128:(k + 1) * 128, :], wcb[i].t[:, 0:n], [wcb[i]], [r_wdn])

                st2 = ExitStack()
                extt = sb(st2, "extt", [H, 768], F32)
                rbt = sb(st2, "rbt", [H, 257], F32)
                hank = sb(st2, "hank", [128, 5, 128], F32)
                DMA(rbt.t[:], relb_d[l], [], [rbt])
                V(lambda: nc.vector.tensor_copy(out=extt.t[:, 0:256], in_=rbt.t[:, 1:257]), [rbt], [extt])
                V(lambda: nc.vector.tensor_copy(out=extt.t[:, 256:768], in_=rbt.t[:, 256:257].broadcast_to([H, 512])),
                  [rbt], [extt])
                DMA(ext_d, extt.t[:], [extt], [r_ext])
                for h in range(H):
                    DMA(hank.t[:], bass.AP(ext_d.tensor, h * 768, [[1, 128], [128, 5], [1, 128]]), [r_ext], [hank])
                    for jj in range(5):
                        bk = banks[0] if jj < 4 else banks[1]
                        col = (jj % 4) * 128
                        PE(lambda: nc.tensor.matmul(bk.t[:, col:col + 128], lhsT=Jf.t[:], rhs=hank.t[:, jj, :], start=True, stop=True),
                           [Jf, hank], [bk])
                    for jj in range(5):
                        bk = banks[0] if jj < 4 else banks[1]
                        col = (jj % 4) * 128
                        A(lambda: nc.scalar.activation(out=Et.t[:, h, 4 - jj, :], in_=bk.t[:, col:col + 128], func=AF.Exp),
                          [bk], [Et])
                    G(lambda: nc.gpsimd.memset(Et.t[0:64, h, 0, 64:128], 0.0), [], [Et])
                    G(lambda: nc.gpsimd.memset(Et.t[64:128, h, 4, 0:64], 0.0), [], [Et])
                S.barrier_all()
                st2.close()
                o_a = sb(st, "o_a", [128, 4, 512], F32)
                o_b = sb(st, "o_b", [128, 4, 512], F32)
                rden = sb(st, "rden", [128, 16], F32)
                G(lambda: nc.gpsimd.memset(qn.t[:, :, 96:128], 0.0), [], [qn])
                G(lambda: nc.gpsimd.memset(kn.t[:, :, 96:128], 0.0), [], [kn])
                for i in range(NRING):
                    G(lambda: nc.gpsimd.memset(caV[i].t[:, :, 64:VW], 1.0), [], [caV[i]])
                for i in range(2):
                    G(lambda: nc.gpsimd.memset(Vh[i].t[:, :, 64:VW], 1.0), [], [Vh[i]])

                if dbg == "s2b":
                    for i in range(DEPTH):
                        DMA(xt[0].t[:, 0:16], w_up_l[i][0:128, 0:16], [], [xt[0]])
                        DMA(xt[0].t[:, 16:32], w_down_l[i][0:128, 0:16], [], [xt[0]])
                    DMA(xt[0].t[0:3, 32:64], conv_w_d[0][:, 0:32], [], [xt[0]])
                    S.barrier_all()
                    return nc
                if dbg == "s2":
                    print("nops at s2", S.nops)
                    S.barrier_all()
                    return nc
                DMA(xt[0].t[:], x_src[0:128, :], [r_xsrc[0]], [xt[0]])

                def prep(t):
                    s4 = t % 4
                    xs = xt[t % 2]
                    if t + 1 < NT:
                        DMA(xt[(t + 1) % 2].t[:], x_src[(t + 1) * 128:(t + 2) * 128, :], [r_xsrc[t + 1]], [xt[(t + 1) % 2]])
                    A(lambda: nc.scalar.activation(out=junk.t[:], in_=xs.t[:], func=AF.Square, scale=1.0 / 32.0,
                                                   accum_out=stat.t[:, 0:1]), [xs], [junk, stat])
                    rstd_of(stat.t[:, 0:1], rs.t[:, 0:1], 1.0, [stat], [rs], stmp)
                    V(lambda: nc.vector.tensor_scalar(out=xb.t[:], in0=xs.t[:], scalar1=rs.t[:, 0:1], scalar2=None, op0=ALU.mult),
                      [xs, rs], [xb])
                    for k in range(8):
                        PE(lambda: nc.tensor.transpose(out=bf(banks[0])[:, k * 128:(k + 1) * 128], in_=xb.t[:, k * 128:(k + 1) * 128],
                                                       identity=ident_b.t[:]), [xb, ident_b], [banks[0]], inc=(k == 7))
                    A(lambda: nc.scalar.copy(out=xT.t[:].rearrange("p k t -> p (k t)"), in_=bf(banks[0])), [banks[0]], [xT])
                    blocks = [(0, 512), (512, 512), (1024, 512), (1536, 416)]
                    for bi, (c0, n) in enumerate(blocks):
                        bk = banks[1 + (bi % 2)]
                        for k in range(8):
                            PE(lambda: nc.tensor.matmul(bk.t[:, 0:n], lhsT=xT.t[:, k, :], rhs=Win.t[:, k, c0:c0 + n],
                                                        start=(k == 0), stop=(k == 7)), [xT, Win], [bk], inc=(k == 7))
                        if bi % 2 == 0:
                            A(lambda: nc.scalar.copy(out=proj.t[:, c0:c0 + n], in_=bk.t[:, 0:n]), [bk], [proj])
                        else:
                            V(lambda: nc.vector.tensor_copy(out=proj.t[:, c0:c0 + n], in_=bk.t[:, 0:n]), [bk], [proj])
                    A(lambda: nc.scalar.activation(out=junk.t[:, 0:256], in_=proj.t[:, 0:256], func=AF.Square, scale=1.0 / 16.0,
                                                   accum_out=stat.t[:, 1:2]), [proj], [junk, stat])
                    A(lambda: nc.scalar.activation(out=junk.t[:, 256:384], in_=proj.t[:, 256:384], func=AF.Square,
                                                   scale=float(128 ** -0.5), accum_out=stat.t[:, 2:3]), [proj], [junk, stat])
                    rstd_of(stat.t[:, 1:3], rs.t[:, 1:3], 1.0, [stat], [rs], stmp)
                    V(lambda: nc.vector.tensor_scalar(out=cb16.t[:, 0:256], in0=proj.t[:, 0:256], scalar1=rs.t[:, 1:2], scalar2=None,
                                                      op0=ALU.mult), [proj, rs], [cb16])
                    V(lambda: nc.vector.tensor_scalar(out=cb16.t[:, 256:384], in0=proj.t[:, 256:384], scalar1=rs.t[:, 2:3], scalar2=None,
                                                      op0=ALU.mult), [proj, rs], [cb16])
                    for k in range(3):
                        PE(lambda: nc.tensor.transpose(out=bf(banks[0])[:, k * 128:(k + 1) * 128], in_=cb16.t[:, k * 128:(k + 1) * 128],
                                                       identity=ident_b.t[:]), [cb16, ident_b], [banks[0]], inc=(k == 2))
                    A(lambda: nc.scalar.copy(out=cT.t[:].rearrange("p k t -> p (k t)"), in_=bf(banks[0])[:, 0:384]), [banks[0]], [cT])
                    for (c0, n, bk) in ((0, 512, banks[1]), (512, 256, banks[2])):
                        for k in range(2):
                            PE(lambda: nc.tensor.matmul(bk.t[:, 0:n], lhsT=cT.t[:, k, :], rhs=Wuq.t[:, k, c0:c0 + n],
                                                        start=(k == 0), stop=(k == 1)), [cT, Wuq], [bk], inc=(k == 1))
                    qf = q_sb.t[:].rearrange("p h d -> p (h d)")
                    A(lambda: nc.scalar.copy(out=qf[:, 0:512], in_=banks[1].t[:, 0:512]), [banks[1]], [q_sb])
                    V(lambda: nc.vector.tensor_copy(out=qf[:, 512:768], in_=banks[2].t[:, 0:256]), [banks[2]], [q_sb])
                    for (c0, bk) in ((0, banks[1]), (512, banks[2])):
                        PE(lambda: nc.tensor.matmul(bk.t[:, 0:512], lhsT=cT.t[:, 2, :], rhs=Wukv.t[:, c0:c0 + 512],
                                                    start=True, stop=True), [cT, Wukv], [bk])
                    kvf = kv_sb.t[:].rearrange("p h d -> p (h d)")
                    A(lambda: nc.scalar.copy(out=kvf[:, 0:512], in_=banks[1].t[:, 0:512]), [banks[1]], [kv_sb])
                    V(lambda: nc.vector.tensor_copy(out=kvf[:, 512:1024], in_=banks[2].t[:, 0:512]), [banks[2]], [kv_sb])
                    sq3 = sq.t[:, 0:768].rearrange("p (h d) -> p h d", d=96)
                    V(lambda: nc.vector.tensor_tensor(out=sq3, in0=q_sb.t[:], in1=q_sb.t[:], op=ALU.mult), [q_sb], [sq])
                    V(lambda: nc.vector.tensor_reduce(out=stat.t[:, 8:16], in_=sq3, axis=AX.X, op=ALU.add), [sq], [stat])
                    sqk = sq.t[:, 0:512].rearrange("p (h d) -> p h d", d=64)
                    V(lambda: nc.vector.tensor_tensor(out=sqk, in0=kv_sb.t[:, :, 0:64], in1=kv_sb.t[:, :, 0:64], op=ALU.mult),
                      [kv_sb], [sq])
                    V(lambda: nc.vector.tensor_reduce(out=stat.t[:, 16:24], in_=sqk, axis=AX.X, op=ALU.add), [sq], [stat])
                    A(lambda: nc.scalar.activation(out=junk.t[:, 0:32], in_=proj.t[:, 384:416], func=AF.Square,
                                                   accum_out=stat.t[:, 3:4]), [proj], [junk, stat])
                    V(lambda: nc.vector.tensor_scalar(out=stat.t[:, 16:24], in0=stat.t[:, 16:24], scalar1=stat.t[:, 3:4], scalar2=None,
                                                      op0=ALU.add), [stat], [stat])
                    caqk = proj.t[:, 416:1440].rearrange("p (h d) -> p h d", d=64)
                    sqc = sq.t[:, 0:1024].rearrange("p (h d) -> p h d", d=64)
                    V(lambda: nc.vector.tensor_tensor(out=sqc, in0=caqk, in1=caqk, op=ALU.mult), [proj], [sq])
                    V(lambda: nc.vector.tensor_reduce(out=stat.t[:, 24:40], in_=sqc, axis=AX.X, op=ALU.add), [sq], [stat])
                    rstd_of(stat.t[:, 8:24], rs.t[:, 8:24], 1.0 / 96.0, [stat], [rs], stmp)
                    rstd_of(stat.t[:, 24:40], rs.t[:, 24:40], 1.0 / 64.0, [stat], [rs], stmp)
                    V(lambda: nc.vector.tensor_tensor(out=q_sb.t[:], in0=q_sb.t[:],
                                                      in1=rs.t[:, 8:16].unsqueeze(2).broadcast_to([128, H, 96]), op=ALU.mult),
                      [q_sb, rs], [q_sb])
                    V(lambda: nc.vector.tensor_tensor(out=qn.t[:, :, 0:64], in0=q_sb.t[:, :, 0:64],
                                                      in1=gq_b.t[:, 0:64].unsqueeze(1).broadcast_to([128, H, 64]), op=ALU.mult),
                      [q_sb, gq_b], [qn])
                    V(lambda: nc.vector.tensor_tensor(out=qrope.t[:], in0=q_sb.t[:, :, 64:96],
                                                      in1=gq_b.t[:, 64:96].unsqueeze(1).broadcast_to([128, H, 32]), op=ALU.mult),
                      [q_sb, gq_b], [qrope])
                    cb_ = cos_t.t[:, t, :].unsqueeze(1).broadcast_to([128, H, 16])
                    sb_ = sin_t.t[:, t, :].unsqueeze(1).broadcast_to([128, H, 16])
                    t1 = qrope.t[:, :, 0:16]
                    t2 = qrope.t[:, :, 16:32]
                    G(lambda: nc.gpsimd.tensor_tensor(out=rtmp.t[:, 0], in0=t1, in1=cb_, op=ALU.mult), [qrope, cos_t], [rtmp])
                    G(lambda: nc.gpsimd.tensor_tensor(out=rtmp.t[:, 1], in0=t2, in1=sb_, op=ALU.mult), [qrope, sin_t], [rtmp])
                    G(lambda: nc.gpsimd.tensor_tensor(out=rtmp.t[:, 2], in0=t1, in1=sb_, op=ALU.mult), [qrope, sin_t], [rtmp])
                    G(lambda: nc.gpsimd.tensor_tensor(out=rtmp.t[:, 3], in0=t2, in1=cb_, op=ALU.mult), [qrope, cos_t], [rtmp])
                    G(lambda: nc.gpsimd.tensor_tensor(out=qn.t[:, :, 64:80], in0=rtmp.t[:, 0], in1=rtmp.t[:, 1], op=ALU.subtract),
                      [rtmp], [qn])
                    G(lambda: nc.gpsimd.tensor_tensor(out=qn.t[:, :, 80:96], in0=rtmp.t[:, 2], in1=rtmp.t[:, 3], op=ALU.add),
                      [rtmp], [qn])
                    V(lambda: nc.vector.tensor_tensor(out=kv_sb.t[:, :, 0:64], in0=kv_sb.t[:, :, 0:64],
                                                      in1=rs.t[:, 16:24].unsqueeze(2).broadcast_to([128, H, 64]), op=ALU.mult),
                      [kv_sb, rs], [kv_sb])
                    V(lambda: nc.vector.tensor_tensor(out=kn.t[:, :, 0:64], in0=kv_sb.t[:, :, 0:64],
                                                      in1=gk_b.t[:, 0:64].unsqueeze(1).broadcast_to([128, H, 64]), op=ALU.mult),
                      [kv_sb, gk_b], [kn])
                    G(lambda: nc.gpsimd.tensor_tensor(out=krg.t[:], in0=proj.t[:, 384:416], in1=gk_b.t[:, 64:96], op=ALU.mult),
                      [proj, gk_b], [krg])
                    c1 = cos_t.t[:, t, :]
                    s1 = sin_t.t[:, t, :]
                    G(lambda: nc.gpsimd.tensor_tensor(out=ktmp.t[:, 0], in0=krg.t[:, 0:16], in1=c1, op=ALU.mult), [krg, cos_t], [ktmp])
                    G(lambda: nc.gpsimd.tensor_tensor(out=ktmp.t[:, 1], in0=krg.t[:, 16:32], in1=s1, op=ALU.mult), [krg, sin_t], [ktmp])
                    G(lambda: nc.gpsimd.tensor_tensor(out=ktmp.t[:, 2], in0=krg.t[:, 0:16], in1=s1, op=ALU.mult), [krg, sin_t], [ktmp])
                    G(lambda: nc.gpsimd.tensor_tensor(out=ktmp.t[:, 3], in0=krg.t[:, 16:32], in1=c1, op=ALU.mult), [krg, cos_t], [ktmp])
                    G(lambda: nc.gpsimd.tensor_tensor(out=krr.t[:, 0:16], in0=ktmp.t[:, 0], in1=ktmp.t[:, 1], op=ALU.subtract),
                      [ktmp], [krr])
                    G(lambda: nc.gpsimd.tensor_tensor(out=krr.t[:, 16:32], in0=ktmp.t[:, 2], in1=ktmp.t[:, 3], op=ALU.add),
                      [ktmp], [krr])
                    V(lambda: nc.vector.tensor_tensor(out=kn.t[:, :, 64:96], in0=krr.t[:, :].unsqueeze(1).broadcast_to([128, H, 32]),
                                                      in1=rs.t[:, 16:24].unsqueeze(2).broadcast_to([128, H, 32]), op=ALU.mult),
                      [krr, rs], [kn])
                    A(lambda: nc.scalar.copy(out=vb.t[:], in_=kv_sb.t[:, :, 64:128]), [kv_sb], [vb])
                    caqk_v = sq.t[:, 0:1024].rearrange("p (h d) -> p h d", d=64)
                    V(lambda: nc.vector.tensor_tensor(out=caqk_v, in0=caqk,
                                                      in1=rs.t[:, 24:40].unsqueeze(2).broadcast_to([128, 16, 64]), op=ALU.mult),
                      [proj, rs], [sq])
                    V(lambda: nc.vector.tensor_tensor(out=caqn.t[:], in0=caqk_v[:, 0:8, :],
                                                      in1=gcq_b.t[:, :].unsqueeze(1).broadcast_to([128, H, 64]), op=ALU.mult),
                      [sq, gcq_b], [caqn])
                    V(lambda: nc.vector.tensor_tensor(out=cakn.t[:], in0=caqk_v[:, 8:16, :],
                                                      in1=gck_b.t[:, :].unsqueeze(1).broadcast_to([128, H, 64]), op=ALU.mult),
                      [sq, gck_b], [cakn])
                    slot = t % NRING
                    A(lambda: nc.scalar.copy(out=caV[slot].t[:, :, 0:64],
                                             in_=proj.t[:, 1440:1952].rearrange("p (h d) -> p h d", d=64)), [proj], [caV[slot]])
                    for h in range(H):
                        PE(lambda: nc.tensor.transpose(out=bf(banks[0])[:, h * 128:(h + 1) * 128], in_=qn.t[:, h, :],
                                                       identity=ident_b.t[:]), [qn, ident_b], [banks[0]], inc=(h == H - 1))
                    A(lambda: nc.scalar.copy(out=qT_st.t[:, :, s4 * 128:(s4 + 1) * 128],
                                             in_=bf(banks[0]).rearrange("p (h t) -> p h t", h=H)), [banks[0]], [qT_st])
                    for h in range(H):
                        PE(lambda: nc.tensor.transpose(out=bf(banks[1])[:, h * 128:(h + 1) * 128], in_=kn.t[:, h, :],
                                                       identity=ident_b.t[:]), [kn, ident_b], [banks[1]], inc=(h == H - 1))
                    ks = kT_stg[t % 2]
                    V(lambda: nc.vector.tensor_copy(out=ks.t[:], in_=bf(banks[1]).rearrange("p (h t) -> p h t", h=H)),
                      [banks[1]], [ks])
                    for h in range(H):
                        DMA(kT_d[h, :, t * 128:(t + 1) * 128], ks.t[:, h, :], [ks], [r_kv[t // 4]])
                        DMA(v_d[h, :, t, :], vb.t[:, h, :], [vb], [r_kv[t // 4]])
                    for pr in range(4):
                        PE(lambda: nc.tensor.transpose(out=bf(banks[2])[:, pr * 128:(pr + 1) * 128],
                                                       in_=caqn.t[:, 2 * pr:2 * pr + 2, :].rearrange("p h d -> p (h d)"),
                                                       identity=ident_b.t[:]), [caqn, ident_b], [banks[2]], inc=(pr == 3))
                    for pr in range(4):
                        PE(lambda: nc.tensor.transpose(out=bf(banks[3])[:, pr * 128:(pr + 1) * 128],
                                                       in_=cakn.t[:, 2 * pr:2 * pr + 2, :].rearrange("p h d -> p (h d)"),
                                                       identity=ident_b.t[:]), [cakn, ident_b], [banks[3]], inc=(pr == 3))
                    A(lambda: nc.scalar.copy(out=caqT.t[:].rearrange("p a t -> p (a t)"), in_=bf(banks[2])[:, 0:512]), [banks[2]], [caqT])
                    V(lambda: nc.vector.tensor_copy(out=cakT[slot].t[:].rearrange("p a t -> p (a t)"), in_=bf(banks[3])[:, 0:512]),
                      [banks[3]], [cakT[slot]])
                    convert_piece()
                    convert_piece()
                    convert_piece()

                def chunk_attn(t):
                    s4 = t % 4
                    js = [j for j in range(5) if t - 4 + j >= 0]
                    for h in range(H):
                        hp, pr = (h % 2) * 64, h // 2
                        sA = banks[1 + (h % 2) * 2]
                        sB = banks[2 + (h % 2) * 2]
                        cp = caP[h % 2]
                        ce = caPe[h % 2]
                        for j in js:
                            kt = t - 4 + j
                            bk = sA if j < 4 else sB
                            col = (j % 4) * 128
                            PE(lambda: nc.tensor.matmul(bk.t[:, col:col + 128], lhsT=cakT[kt % NRING].t[hp:hp + 64, pr, :],
                                                        rhs=caqT.t[hp:hp + 64, pr, :], start=True, stop=True),
                               [cakT[kt % NRING], caqT], [bk], inc=(j == js[-1] or j == 3))
                        jA = [j for j in js if j < 4]
                        if jA:
                            j0 = jA[0]
                            A(lambda: nc.scalar.activation(out=cp.t[:, j0:4, :].rearrange("p j t -> p (j t)"),
                                                           in_=sA.t[:, j0 * 128:512], func=AF.Exp), [sA], [cp])
                        A(lambda: nc.scalar.activation(out=cp.t[:, 4, :], in_=sB.t[:, 0:128], func=AF.Exp), [sB], [cp])
                        j0 = js[0]
                        V(lambda: nc.vector.tensor_tensor(out=ce.t[:, j0:5, :], in0=cp.t[:, j0:5, :], in1=Et.t[:, h, j0:5, :], op=ALU.mult),
                          [cp, Et], [ce])
                        ob = banks[5 + (h // 4)]
                        for j in js:
                            kt = t - 4 + j
                            PE(lambda: nc.tensor.matmul(ob.t[:, (h % 4) * VW:(h % 4) * VW + VW], lhsT=ce.t[:, j, :],
                                                        rhs=caV[kt % NRING].t[:, h, :], start=(j == js[0]), stop=(j == js[-1])),
                               [ce, caV[kt % NRING]], [ob], inc=(j == js[-1]))
                    for half in range(2):
                        ob = banks[5 + half]
                        o3 = ob.t[:, 0:4 * VW].rearrange("p (h d) -> p h d", d=VW)
                        V(lambda: nc.vector.reciprocal(out=rden.t[:, half * 4:half * 4 + 4], in_=o3[:, :, 64:65].rearrange("p h o -> p (h o)")),
                          [ob], [rden])
                        V(lambda: nc.vector.tensor_tensor(out=o_b.t[:, s4, half * 256:(half + 1) * 256].rearrange("p (h d) -> p h d", d=64),
                                                          in0=o3[:, :, 0:64],
                                                          in1=rden.t[:, half * 4:half * 4 + 4].unsqueeze(2).broadcast_to([128, 4, 64]),
                                                          op=ALU.mult), [ob, rden], [o_b])

                def mla_attn(T):
                    nk = 4 * T + 4
                    kvr = [r_kv[i] for i in range(T + 1)]

                    def load_head(h):
                        DMA(kTh[h % 2].t[:, 0:nk * 128], kT_d[h, :, 0:nk * 128], kvr, [kTh[h % 2]])
                        DMA(Vh[h % 2].t[:, 0:nk, 0:64], v_d[h, :, 0:nk, :], kvr, [Vh[h % 2]])

                    load_head(0)
                    sctr = [0]
                    for h in range(H):
                        if h + 1 < H:
                            load_head(h + 1)
                        kb = kTh[h % 2]
                        vh = Vh[h % 2]
                        pend = None
                        for step in range(nk + 1):
                            if step < nk:
                                kt = step
                                i = kt - 4 * T
                                q0 = max(i, 0) * 128
                                n = 512 - q0
                                sbk = banks[1 + (sctr[0] % 2)]
                                pt = PT[sctr[0] % 4]
                                sctr[0] += 1
                                PE(lambda: nc.tensor.matmul(sbk.t[:, 0:n], lhsT=kb.t[:, kt * 128:(kt + 1) * 128],
                                                            rhs=qT_st.t[:, h, q0:512], start=True, stop=True),
                                   [kb, qT_st], [sbk])
                                if i >= 0:
                                    A(lambda: nc.scalar.activation(out=pt.t[:, q0:q0 + 64], in_=sbk.t[:, 0:64], func=AF.Exp,
                                                                   bias=maskb.t[:, 0:1]), [sbk, maskb], [pt])
                                    A(lambda: nc.scalar.activation(out=pt.t[:, q0 + 64:512], in_=sbk.t[:, 64:n], func=AF.Exp),
                                      [sbk], [pt])
                                else:
                                    A(lambda: nc.scalar.activation(out=pt.t[:, 0:512], in_=sbk.t[:, 0:512], func=AF.Exp), [sbk], [pt])
                                new_pend = (kt, pt, q0 // 128)
                            else:
                                new_pend = None
                            if pend is not None:
                                pkt, ppt, qs0 = pend
                                for qs in range(qs0, 4):
                                    ob = banks[4 + qs]
                                    last = 4 * T + qs
                                    PE(lambda: nc.tensor.matmul(ob.t[:, 0:VW], lhsT=ppt.t[:, qs * 128:(qs + 1) * 128],
                                                                rhs=vh.t[:, pkt, :], start=(pkt == 0), stop=(pkt == last)),
                                       [ppt, vh], [ob], inc=(pkt == last or qs == 3))
                            pend = new_pend
                        for qs in range(4):
                            ob = banks[4 + qs]
                            V(lambda: nc.vector.reciprocal(out=rden.t[:, 8 + qs:9 + qs], in_=ob.t[:, 64:65]), [ob], [rden])
                            V(lambda: nc.vector.tensor_scalar(out=o_a.t[:, qs, h * 64:(h + 1) * 64], in0=ob.t[:, 0:64],
                                                              scalar1=rden.t[:, 8 + qs:9 + qs], scalar2=None, op0=ALU.mult),
                              [ob, rden], [o_a])

                def out_proj(t):
                    s4 = t % 4
                    xres = xr[0]
                    DMA(xres.t[:], x_src[t * 128:(t + 1) * 128, :], [r_xsrc[t]], [xres])
                    A(lambda: nc.scalar.activation(out=junk.t[:, 0:512], in_=o_a.t[:, s4, :], func=AF.Square, scale=float(512 ** -0.5),
                                                   accum_out=stat.t[:, 40:41]), [o_a], [junk, stat])
                    A(lambda: nc.scalar.activation(out=junk.t[:, 512:1024], in_=o_b.t[:, s4, :], func=AF.Square, scale=float(512 ** -0.5),
                                                   accum_out=stat.t[:, 41:42]), [o_b], [junk, stat])
                    rstd_of(stat.t[:, 40:42], rs.t[:, 40:42], 1.0, [stat], [rs], stmp)
                    V(lambda: nc.vector.tensor_scalar(out=og.t[:, 0:512], in0=o_a.t[:, s4, :], scalar1=rs.t[:, 40:41], scalar2=None,
                                                      op0=ALU.mult), [o_a, rs], [og])
                    V(lambda: nc.vector.tensor_scalar(out=og.t[:, 512:1024], in0=o_b.t[:, s4, :], scalar1=rs.t[:, 41:42], scalar2=None,
                                                      op0=ALU.mult), [o_b, rs], [og])
                    for k in range(8):
                        PE(lambda: nc.tensor.transpose(out=bf(banks[0])[:, k * 128:(k + 1) * 128], in_=og.t[:, k * 128:(k + 1) * 128],
                                                       identity=ident_b.t[:]), [og, ident_b], [banks[0]], inc=(k == 7))
                    A(lambda: nc.scalar.copy(out=oT.t[:].rearrange("p k t -> p (k t)"), in_=bf(banks[0])), [banks[0]], [oT])
                    ys = xres
                    for nh in range(2):
                        bk = banks[1 + nh]
                        for k in range(8):
                            PE(lambda: nc.tensor.matmul(bk.t[:, 0:512], lhsT=oT.t[:, k, :], rhs=Wout.t[:, k, nh * 512:(nh + 1) * 512],
                                                        start=(k == 0), stop=(k == 7)), [oT, Wout], [bk], inc=(k == 7))
                        V(lambda: nc.vector.tensor_tensor(out=ys.t[:, nh * 512:(nh + 1) * 512], in0=bk.t[:, 0:512],
                                                          in1=xres.t[:, nh * 512:(nh + 1) * 512], op=ALU.add), [bk, xres], [ys])
                    DMA(y_d[t * 128:(t + 1) * 128, :], ys.t[:], [ys], [r_y[t]])

                for T in range(NST):
                    for s4 in range(4):
                        prep(4 * T + s4)
                        if dbg == "s3":
                            print("nops at s3", S.nops)
                            S.barrier_all()
                            return nc
                        chunk_attn(4 * T + s4)
                        if dbg == "s4":
                            S.barrier_all()
                            return nc
                    mla_attn(T)
                    if dbg == "s5":
                        S.barrier_all()
                        return nc
                    for s4 in range(4):
                        out_proj(4 * T + s4)
                while ffn_pieces:
                    convert_piece()
                S.barrier_all()

            if dbg == "attn" and l == NL - 1:
                break

            with ExitStack() as st:
                Wdn = sb(st, "Wdn", [128, NCH, D], BF16)
                hT = sb(st, "hT", [128, NCH, 512], BF16)
                xnT = sb(st, "xnT", [128, 8, 512], BF16)
                x4 = [sb(st, f"x4_{i}", [128, D], F32) for i in range(4)]
                xb = sb(st, "xbB", [128, D], BF16)
                junk = sb(st, "junkB", [128, D], BF16)
                stat = sb(st, "statB", [128, 8], F32)
                stmp = sb(st, "stmpB", [128, 8], F32)
                rs = sb(st, "rsB", [128, 8], F32)
                wu = [sb(st, f"wu{i}", [128, 8, 2, 128], BF16) for i in range(3)]
                pre = [sb(st, f"pre{i}", [128, 514], F32) for i in range(4)]
                y0 = [sb(st, f"y0_{i}", [128, 512], F32) for i in range(4)]
                y1 = [sb(st, f"y1_{i}", [128, 512], F32) for i in range(4)]
                sg = [sb(st, f"sg{i}", [128, 512], F32) for i in range(2)]
                halo = sb(st, "halo", [128, 2 * NCH, 2], F32)
                cwr = sb(st, "cwr", [128, 128], F32)
                cwr2 = sb(st, "cwr2", [64, 128], F32)
                cw = sb(st, "cw", [128, 176], F32)
                yo = [sb(st, f"yo{i}", [128, 512], F32) for i in range(2)]

                wdv = wdn_s.rearrange("(c p) n -> p c n", p=128)
                DMA(Wdn.t[:, 0:11, :], wdv[:, 0:11, :], [r_wdn], [Wdn])
                DMA(Wdn.t[:, 11:22, :], wdv[:, 11:22, :], [r_wdn], [Wdn])
                cwv = conv_w_d[l].rearrange("i (c p) -> (i c) p", p=128)
                DMA(cwr.t[:], cwv[0:128, :], [], [cwr])
                G(lambda: nc.gpsimd.memset(cwr2.t[:], 0.0), [], [cwr2])
                DMA(cwr2.t[0:4, :], cwv[128:132, :], [], [cwr2])
                DMA(cwr2.t[4:48, :], conv_b_d[l].rearrange("(c p) -> c p", p=128), [], [cwr2])
                PE(lambda: nc.tensor.transpose(out=banks[0].t[:, 0:128], in_=cwr.t[:], identity=ident_f.t[:]), [cwr, ident_f], [banks[0]])
                PE(lambda: nc.tensor.transpose(out=banks[0].t[:, 128:192], in_=cwr2.t[:], identity=ident_f.t[0:64, 0:64]),
                   [cwr2, ident_f], [banks[0]])
                V(lambda: nc.vector.tensor_copy(out=cw.t[:], in_=banks[0].t[:, 0:176]), [banks[0]], [cw])
                G(lambda: nc.gpsimd.memset(halo.t[:], 0.0), [], [halo])

                def wup_load(c, slot):
                    DMA(wu[slot].t[:, :, 0, :], wup_s[:, c * 128:(c + 1) * 128].rearrange("(k p) n -> p k n", p=128), [r_wup], [wu[slot]])
                    DMA(wu[slot].t[:, :, 1, :], wup_s[:, DFF + c * 128:DFF + (c + 1) * 128].rearrange("(k p) n -> p k n", p=128),
                        [r_wup], [wu[slot]])

                def ffn_norm(m):
                    for s4 in range(4):
                        t = 4 * m + s4
                        xs = x4[s4]
                        DMA(xs.t[:], y_d[t * 128:(t + 1) * 128, :], [r_y[t]], [xs])
                        A(lambda: nc.scalar.activation(out=junk.t[:], in_=xs.t[:], func=AF.Square, scale=1.0 / 32.0,
                                                       accum_out=stat.t[:, s4:s4 + 1]), [xs], [junk, stat])
                        rstd_of(stat.t[:, s4:s4 + 1], rs.t[:, s4:s4 + 1], 1.0, [stat], [rs], stmp)
                        V(lambda: nc.vector.tensor_scalar(out=xb.t[:], in0=xs.t[:], scalar1=rs.t[:, s4:s4 + 1], scalar2=None, op0=ALU.mult),
                          [xs, rs], [xb])
                        for k in range(8):
                            PE(lambda: nc.tensor.transpose(out=bf(banks[0])[:, k * 128:(k + 1) * 128], in_=xb.t[:, k * 128:(k + 1) * 128],
                                                           identity=ident_b.t[:]), [xb, ident_b], [banks[0]], inc=(k == 7))
                        V(lambda: nc.vector.tensor_copy(out=xnT.t[:, :, s4 * 128:(s4 + 1) * 128],
                                                        in_=bf(banks[0]).rearrange("p (k t) -> p k t", k=8)), [banks[0]], [xnT])

                gctr = [0]
                for m in range(NST):
                    ffn_norm(m)
                    wup_load(0, gctr[0] % 3)
                    wup_load(1, (gctr[0] + 1) % 3)
                    for c in range(NCH):
                        ws = wu[gctr[0] % 3]
                        if c + 2 < NCH:
                            wup_load(c + 2, (gctr[0] + 2) % 3)
                        par = gctr[0] % 2
                        gctr[0] += 1
                        res = []
                        for gu in range(2):
                            bk = banks[1 + par * 2 + gu]
                            for k in range(8):
                                PE(lambda: nc.tensor.matmul(bk.t[:, 0:512], lhsT=ws.t[:, k, gu, :], rhs=xnT.t[:, k, :],
                                                            start=(k == 0), stop=(k == 7)), [ws, xnT], [bk], inc=(k == 7))
                            pr_ = pre[par * 2 + gu]
                            a0 = y0[par * 2 + gu]
                            a1 = y1[par * 2 + gu]
                            ch = gu * NCH + c
                            G(lambda: nc.gpsimd.tensor_copy(out=pr_.t[:, 0:2], in_=halo.t[:, ch, :]), [halo], [pr_])
                            A(lambda: nc.scalar.copy(out=pr_.t[:, 2:514], in_=bk.t[:, 0:512]), [bk], [pr_])
                            A(lambda: nc.scalar.activation(out=a0.t[:], in_=bk.t[:, 0:512], func=AF.Identity,
                                                           scale=cw.t[:, 2 * 44 + ch:2 * 44 + ch + 1],
                                                           bias=cw.t[:, 132 + ch:133 + ch]), [bk, cw], [a0])
                            G(lambda: nc.gpsimd.tensor_copy(out=halo.t[:, ch, :], in_=pr_.t[:, 512:514]), [pr_], [halo])
                            V(lambda: nc.vector.scalar_tensor_tensor(out=a1.t[:], in0=pr_.t[:, 1:513], scalar=cw.t[:, 44 + ch:45 + ch],
                                                                     in1=a0.t[:], op0=ALU.mult, op1=ALU.add), [pr_, cw, a0], [a1])
                            V(lambda: nc.vector.scalar_tensor_tensor(out=a0.t[:], in0=pr_.t[:, 0:512], scalar=cw.t[:, ch:ch + 1],
                                                                     in1=a1.t[:], op0=ALU.mult, op1=ALU.add), [pr_, cw, a1], [a0])
                            res.append(a0)
                        sgb = sg[par]
                        A(lambda: nc.scalar.activation(out=sgb.t[:], in_=res[0].t[:], func=AF.Silu), [res[0]], [sgb])
                        V(lambda: nc.vector.tensor_tensor(out=hT.t[:, c, :], in0=sgb.t[:], in1=res[1].t[:], op=ALU.mult),
                          [sgb, res[1]], [hT])
                    for s4 in range(4):
                        t = 4 * m + s4
                        for nh in range(2):
                            bk = banks[5 + nh]
                            for c in range(NCH):
                                PE(lambda: nc.tensor.matmul(bk.t[:, 0:512], lhsT=hT.t[:, c, s4 * 128:(s4 + 1) * 128],
                                                            rhs=Wdn.t[:, c, nh * 512:(nh + 1) * 512], start=(c == 0), stop=(c == NCH - 1)),
                                   [hT, Wdn], [bk], inc=(c == NCH - 1))
                            yb = yo[nh]
                            V(lambda: nc.vector.tensor_tensor(out=yb.t[:], in0=bk.t[:, 0:512], in1=x4[s4].t[:, nh * 512:(nh + 1) * 512],
                                                              op=ALU.add), [bk, x4[s4]], [yb])
                            DMA(y_d[t * 128:(t + 1) * 128, nh * 512:(nh + 1) * 512], yb.t[:], [yb], [r_y[t]])
                S.barrier_all()

        S.finish(r_y)
        S.barrier_all()
        print(f"[build] instrs: " + ", ".join(f"{k}={e.count}" for k, e in S.engs.items()) + f" waits={S.nwait}", flush=True)
    return nc


_NAMES = ["g_mix", "w_in", "w_uq", "w_ukv", "g_q_lora", "g_kv_lora", "g_mla_q", "g_mla_k", "g_ca_q", "g_ca_k",
          "rel_bias", "g_out_mla", "g_out_ca", "w_out", "g_ffn", "w_up", "conv_w", "conv_b", "w_down"]


def kernel(**inputs):
    x = np.ascontiguousarray(inputs["x"], dtype=np.float32)
    pos = np.ascontiguousarray(inputs["positions"], dtype=np.int32)
    B = x.shape[0]
    shared = {n: np.ascontiguousarray(inputs[n], dtype=np.float32) for n in _NAMES if n not in ("w_up", "w_down")}
    for i in range(DEPTH):
        shared[f"w_up_l{i}"] = np.ascontiguousarray(inputs["w_up"][i], dtype=np.float32)
        shared[f"w_down_l{i}"] = np.ascontiguousarray(inputs["w_down"][i], dtype=np.float32)
    nc = build()
    in_maps = []
    for b in range(B):
        m = dict(shared)
        m["x"] = x[b]
        m["positions"] = pos[b]
        in_maps.append(m)
    res = run_bass_kernel_spmd(nc, in_maps, core_ids=list(range(B)))
    return np.stack([np.asarray(r["y"]) for r in res.results], axis=0).astype(np.float32)
```

```python
import numpy as np
from contextlib import ExitStack
import concourse.bass as bass
import concourse.mybir as mybir
from concourse.bass_utils import run_bass_kernel_spmd

F32 = mybir.dt.float32
BF16 = mybir.dt.bfloat16
I32 = mybir.dt.int32
AF = mybir.ActivationFunctionType
ALU = mybir.AluOpType
AX = mybir.AxisListType

D = 1024
SEQ = 4096
DEPTH = 4
NT = SEQ // 128
NST = NT // 4
EPS = 1e-6
H = 8
QK = 96
INC = 1952
DFF = 2816
NCH = DFF // 128
EPOCH = 30000
NRING = 6
VW = 72


class Res:
    __slots__ = ("w", "r")

    def __init__(self):
        self.w = None
        self.r = {}


class Buf:
    def __init__(self, t):
        self.t = t
        self.r = Res()


class Eng:
    def __init__(self, S, name, obj):
        self.S = S
        self.name = name
        self.obj = obj
        self.count = 0
        self.sems = []
        self.seen = {}

    def sem_for(self, epoch):
        while len(self.sems) <= epoch:
            self.sems.append(self.S.new_sem(f"{self.name}_e{len(self.sems)}"))
        return self.sems[epoch]


class Sched:
    def __init__(self, nc, stack):
        self.nc = nc
        self.stack = stack
        self.engs = {
            "pe": Eng(self, "pe", nc.tensor),
            "act": Eng(self, "act", nc.scalar),
            "dve": Eng(self, "dve", nc.vector),
            "pool": Eng(self, "pool", nc.gpsimd),
            "sp": Eng(self, "sp", nc.sync),
        }
        self.dma_rings = {}
        self.keysem = {}
        self.nwait = 0
        self.nops = 0
        self.limit = None

    def new_sem(self, name):
        return self.stack.enter_context(self.nc.semaphore(name))

    def _wait(self, eng, ev):
        key, val = ev
        e = self.engs[eng]
        if e.seen.get(key, 0) >= val:
            return
        e.obj.wait_ge(self.keysem[key], val)
        e.seen[key] = val
        self.nwait += 1

    def _deps(self, eng, reads, writes):
        e = self.engs[eng]
        best = {}

        def add(k, v, raw):
            if k[0] == eng:
                if not raw and eng == "pe":
                    return
                if v + k[1] * EPOCH > e.count:
                    return
            if best.get(k, 0) < v:
                best[k] = v

        for r in reads:
            if r.w is not None:
                add(r.w[0], r.w[1], True)
        for w in writes:
            if w.w is not None:
                add(w.w[0], w.w[1], False)
            for k, v in w.r.items():
                add(k, v, False)
        for k, v in best.items():
            self._wait(eng, (k, v))

    def op(self, eng, fn, reads=(), writes=(), inc=True):
        self.nops += 1
        if self.limit is not None and self.nops > self.limit:
            return None
        e = self.engs[eng]
        reads = [b.r if isinstance(b, Buf) else b for b in reads]
        writes = [b.r if isinstance(b, Buf) else b for b in writes]
        self._deps(eng, reads, writes)
        n = e.count + 1
        epoch = (n - 1) // EPOCH
        sem = e.sem_for(epoch)
        key = (eng, epoch)
        self.keysem[key] = sem
        val = n - epoch * EPOCH
        inst = fn()
        if inc:
            inst.then_inc(sem, 1)
            e.count += 1
        for r in reads:
            if r.r.get(key, 0) < val:
                r.r[key] = val
        for w in writes:
            w.w = (key, val)
            w.r = {}
        return inst

    def dma(self, q, out, in_, reads=(), writes=(), **kw):
        self.nops += 1
        if self.limit is not None and self.nops > self.limit:
            return None
        e = self.engs[q]
        reads = [b.r if isinstance(b, Buf) else b for b in reads]
        writes = [b.r if isinstance(b, Buf) else b for b in writes]
        ring = self.dma_rings.setdefault(q, {"sems": [], "vals": [], "i": 0})
        NR = 12
        if len(ring["sems"]) < NR:
            ring["sems"].append(self.new_sem(f"dma_{q}_{len(ring['sems'])}"))
            ring["vals"].append(0)
            idx = len(ring["sems"]) - 1
        else:
            idx = ring["i"] % NR
        ring["i"] += 1
        sem = ring["sems"][idx]
        key = ("dma", q, idx)
        self.keysem[key] = sem
        if ring["vals"][idx] > 0:
            self._wait(q, (key, ring["vals"][idx]))
        self._deps(q, reads, writes)
        inst = e.obj.dma_start(out=out, in_=in_, **kw)
        inst.then_inc(sem, 16)
        ring["vals"][idx] += 16
        val = ring["vals"][idx]
        for r in reads:
            if r.r.get(key, 0) < val:
                r.r[key] = val
        for w in writes:
            w.w = (key, val)
            w.r = {}
        return inst

    def finish(self, res_list):
        for r in res_list:
            r = r.r if isinstance(r, Buf) else r
            if r.w is not None:
                self._wait("sp", r.w)

    def barrier_all(self):
        evs = []
        for name, e in self.engs.items():
            if e.count > 0:
                n = e.count
                epoch = (n - 1) // EPOCH
                evs.append(((name, epoch), n - epoch * EPOCH))
        for q, ring in self.dma_rings.items():
            for idx, v in enumerate(ring["vals"]):
                if v > 0:
                    evs.append((("dma", q, idx), v))
        for name in self.engs:
            for ev in evs:
                self._wait(name, ev)


def build(NL=DEPTH, dbg=None, limit=None):
    nc = bass.Bass("TRN2", target_bir_lowering=False)
    dt_in = lambda name, shape, dt=F32: nc.dram_tensor(name, list(shape), dt, kind="ExternalInput").ap()
    x_d = dt_in("x", [SEQ, D])
    pos_d = dt_in("positions", [SEQ], I32)
    g_mix_d = dt_in("g_mix", [DEPTH, D])
    w_in_d = dt_in("w_in", [DEPTH, D, INC])
    w_uq_d = dt_in("w_uq", [DEPTH, 256, 768])
    w_ukv_d = dt_in("w_ukv", [DEPTH, 128, 1024])
    g_ql_d = dt_in("g_q_lora", [DEPTH, 256])
    g_kvl_d = dt_in("g_kv_lora", [DEPTH, 128])
    g_mq_d = dt_in("g_mla_q", [DEPTH, 96])
    g_mk_d = dt_in("g_mla_k", [DEPTH, 96])
    g_cq_d = dt_in("g_ca_q", [DEPTH, 64])
    g_ck_d = dt_in("g_ca_k", [DEPTH, 64])
    relb_d = dt_in("rel_bias", [DEPTH, H, 257])
    g_om_d = dt_in("g_out_mla", [DEPTH, 512])
    g_oc_d = dt_in("g_out_ca", [DEPTH, 512])
    w_out_d = dt_in("w_out", [DEPTH, D, D])
    g_ffn_d = dt_in("g_ffn", [DEPTH, D])
    w_up_l = [dt_in(f"w_up_l{i}", [D, 2 * DFF]) for i in range(DEPTH)]
    conv_w_d = dt_in("conv_w", [DEPTH, 3, 2 * DFF])
    conv_b_d = dt_in("conv_b", [DEPTH, 2 * DFF])
    w_down_l = [dt_in(f"w_down_l{i}", [DFF, D]) for i in range(DEPTH)]
    y_d = nc.dram_tensor("y", [SEQ, D], F32, kind="ExternalOutput").ap()
    kT_d = nc.dram_tensor("kT_s", [H, 128, SEQ], BF16, kind="ExternalOutput").ap()
    v_d = nc.dram_tensor("v_s", [H, 128, NT, 64], BF16, kind="ExternalOutput").ap()
    ext_d = nc.dram_tensor("ext_s", [H, 768], F32, kind="Internal").ap()
    wup_s = nc.dram_tensor("wup_s", [D, 2 * DFF], BF16, kind="ExternalOutput").ap()
    wdn_s = nc.dram_tensor("wdn_s", [DFF, D], BF16, kind="ExternalOutput").ap()

    with ExitStack() as top:
        S = Sched(nc, top)
        S.limit = limit

        uniq = [0]

        def sb(st, name, shape, dt=F32):
            uniq[0] += 1
            return Buf(st.enter_context(nc.sbuf_tensor(f"{name}_u{uniq[0]}", list(shape), dt)))

        V = lambda fn, r=(), w=(): S.op("dve", fn, r, w)
        A = lambda fn, r=(), w=(): S.op("act", fn, r, w)
        G = lambda fn, r=(), w=(): S.op("pool", fn, r, w)
        PE = lambda fn, r=(), w=(), inc=True: S.op("pe", fn, r, w, inc)
        DMA = lambda out, in_, r=(), w=(), **kw: S.dma("sp", out, in_, r, w, **kw)

        banks = [Buf(top.enter_context(nc.psum_tensor(f"bank{i}", [128, 512], F32))) for i in range(8)]

        def bf(bank):
            return bank.t[:].bitcast(BF16)

        ident_b = sb(top, "ident_b", [128, 128], BF16)
        ident_f = sb(top, "ident_f", [128, 128], F32)
        Jf = sb(top, "Jf", [128, 128], F32)
        ones_f = sb(top, "ones_f", [128, 128], F32)
        eps_t = sb(top, "eps_t", [128, 1], F32)
        maskb = sb(top, "maskb", [128, 1], F32)
        cos_t = sb(top, "cos_t", [128, NT, 16], F32)
        sin_t = sb(top, "sin_t", [128, NT, 16], F32)
        r_y = [Res() for _ in range(NT)]
        r_kv = [Res() for _ in range(NST)]
        r_ext = Res()
        r_wup = Res()
        r_wdn = Res()

        G(lambda: nc.gpsimd.memset(ones_f.t[:], 1.0), [], [ones_f])
        G(lambda: nc.gpsimd.affine_select(out=ident_f.t[:], in_=ones_f.t[:], pattern=[[1, 128]],
                                          compare_op=ALU.is_equal, fill=0.0, base=0, channel_multiplier=-1),
          [ones_f], [ident_f])
        G(lambda: nc.gpsimd.affine_select(out=Jf.t[:], in_=ones_f.t[:], pattern=[[1, 128]],
                                          compare_op=ALU.is_equal, fill=0.0, base=-127, channel_multiplier=1),
          [ones_f], [Jf])
        G(lambda: nc.gpsimd.tensor_copy(out=ident_b.t[:], in_=ident_f.t[:]), [ident_f], [ident_b])
        G(lambda: nc.gpsimd.memset(eps_t.t[:], EPS), [], [eps_t])
        G(lambda: nc.gpsimd.memset(maskb.t[:], 0.0), [], [maskb])
        G(lambda: nc.gpsimd.memset(maskb.t[64:128, :], -30000.0), [], [maskb])

        with ExitStack() as st:
            pos_i = sb(st, "pos_i", [NT, 128], I32)
            pos_f = sb(st, "pos_f", [NT, 128], F32)
            posT = sb(st, "posT", [128, NT], F32)
            invf = sb(st, "invf", [128, 16], F32)
            ang = sb(st, "ang", [128, NT, 16], F32)
            ki = sb(st, "ki", [128, NT, 16], I32)
            kf = sb(st, "kf", [128, NT, 16], F32)
            rr = sb(st, "rr", [128, NT, 16], F32)
            DMA(pos_i.t[:], pos_d.rearrange("(t p) -> t p", p=128), [], [pos_i])
            V(lambda: nc.vector.tensor_copy(out=pos_f.t[:], in_=pos_i.t[:]), [pos_i], [pos_f])
            PE(lambda: nc.tensor.transpose(out=banks[0].t[:, 0:NT], in_=pos_f.t[:], identity=ident_f.t[0:NT, 0:NT]),
               [pos_f, ident_f], [banks[0]])
            V(lambda: nc.vector.tensor_copy(out=posT.t[:], in_=banks[0].t[:, 0:NT]), [banks[0]], [posT])
            for j in range(16):
                G(lambda: nc.gpsimd.memset(invf.t[:, j:j + 1], float(np.float32(10000.0) ** np.float32(-(2.0 * j) / 32.0))),
                  [], [invf])
            V(lambda: nc.vector.tensor_tensor(out=ang.t[:], in0=posT.t[:, :].unsqueeze(2).broadcast_to([128, NT, 16]),
                                              in1=invf.t[:, :].unsqueeze(1).broadcast_to([128, NT, 16]), op=ALU.mult),
              [posT, invf], [ang])
            for (shift, dst) in ((0.0, sin_t), (float(np.pi / 2), cos_t)):
                V(lambda: nc.vector.tensor_scalar(out=ki.t[:], in0=ang.t[:], scalar1=shift, scalar2=float(1.0 / (2 * np.pi)),
                                                  op0=ALU.add, op1=ALU.mult), [ang], [ki])
                V(lambda: nc.vector.tensor_copy(out=kf.t[:], in_=ki.t[:]), [ki], [kf])
                V(lambda: nc.vector.scalar_tensor_tensor(out=rr.t[:], in0=kf.t[:], scalar=-6.28125, in1=ang.t[:],
                                                         op0=ALU.mult, op1=ALU.add), [kf, ang], [rr])
                V(lambda: nc.vector.tensor_scalar(out=rr.t[:], in0=rr.t[:], scalar1=shift, scalar2=None, op0=ALU.add),
                  [rr], [rr])
                V(lambda: nc.vector.scalar_tensor_tensor(out=rr.t[:], in0=kf.t[:], scalar=-0.0019353071795864769,
                                                         in1=rr.t[:], op0=ALU.mult, op1=ALU.add), [kf, rr], [rr])
                V(lambda: nc.vector.tensor_scalar(out=kf.t[:], in0=rr.t[:], scalar1=float(np.pi), scalar2=float(-2 * np.pi),
                                                  op0=ALU.is_gt, op1=ALU.mult), [rr], [kf])
                V(lambda: nc.vector.tensor_tensor(out=rr.t[:], in0=rr.t[:], in1=kf.t[:], op=ALU.add), [rr, kf], [rr])
                V(lambda: nc.vector.tensor_scalar(out=kf.t[:], in0=rr.t[:], scalar1=float(-np.pi), scalar2=float(2 * np.pi),
                                                  op0=ALU.is_lt, op1=ALU.mult), [rr], [kf])
                V(lambda: nc.vector.tensor_tensor(out=rr.t[:], in0=rr.t[:], in1=kf.t[:], op=ALU.add), [rr, kf], [rr])
                A(lambda: nc.scalar.activation(out=dst.t[:], in_=rr.t[:], func=AF.Sin), [rr], [dst])
            S.barrier_all()
        if dbg == "s1":
            return nc

        def rstd_of(ms_ap, out_ap, scale, r, w, tmp):
            n = ms_ap.shape[-1]
            A(lambda: nc.scalar.activation(out=tmp.t[:, 0:n], in_=ms_ap, func=AF.Ln, bias=eps_t.t[:, 0:1], scale=scale),
              list(r) + [eps_t], [tmp])
            A(lambda: nc.scalar.activation(out=out_ap, in_=tmp.t[:, 0:n], func=AF.Exp, scale=-0.5), [tmp], list(w))

        for l in range(NL):
            x_src = x_d if l == 0 else y_d
            r_xsrc = [Res() for _ in range(NT)] if l == 0 else r_y

            with ExitStack() as st:
                Win = sb(st, "Win", [128, 8, INC], BF16)
                Wuq = sb(st, "Wuq", [128, 2, 768], BF16)
                Wukv = sb(st, "Wukv", [128, 1024], BF16)
                Wout = sb(st, "Wout", [128, 8, 1024], BF16)
                wst = [sb(st, f"wst{i}", [128, 1024], F32) for i in range(2)]
                wcb = [sb(st, f"wcb{i}", [128, 1024], BF16) for i in range(2)]
                gpk = sb(st, "gpk", [128, 32], F32)
                gq_b = sb(st, "gq_b", [128, 96], F32)
                gk_b = sb(st, "gk_b", [128, 96], F32)
                gcq_b = sb(st, "gcq_b", [128, 64], F32)
                gck_b = sb(st, "gck_b", [128, 64], F32)
                Et = sb(st, "Et", [128, H, 5, 128], BF16)
                xt = [sb(st, f"xt{i}", [128, D], F32) for i in range(2)]
                xr = [sb(st, "xr0", [128, D], F32)]
                junk = sb(st, "junk", [128, D], BF16)
                xb = sb(st, "xb", [128, D], BF16)
                xT = sb(st, "xT", [128, 8, 128], BF16)
                proj = sb(st, "proj", [128, INC], F32)
                sq = sb(st, "sq", [128, 1024], F32)
                stat = sb(st, "stat", [128, 64], F32)
                stmp = sb(st, "stmp", [128, 64], F32)
                rs = sb(st, "rs", [128, 64], F32)
                cb16 = sb(st, "cb16", [128, 384], BF16)
                cT = sb(st, "cT", [128, 3, 128], BF16)
                q_sb = sb(st, "q_sb", [128, H, 96], F32)
                kv_sb = sb(st, "kv_sb", [128, H, 128], F32)
                qrope = sb(st, "qrope", [128, H, 32], F32)
                rtmp = sb(st, "rtmp", [128, 4, H, 16], F32)
                krg = sb(st, "krg", [128, 32], F32)
                krr = sb(st, "krr", [128, 32], F32)
                ktmp = sb(st, "ktmp", [128, 4, 16], F32)
                qn = sb(st, "qn", [128, H, 128], BF16)
                kn = sb(st, "kn", [128, H, 128], BF16)
                vb = sb(st, "vb", [128, H, 64], BF16)
                caqn = sb(st, "caqn", [128, H, 64], BF16)
                cakn = sb(st, "cakn", [128, H, 64], BF16)
                qT_st = sb(st, "qT_st", [128, H, 512], BF16)
                kT_stg = [sb(st, f"kT_stg{i}", [128, H, 128], BF16) for i in range(2)]
                caqT2 = [sb(st, f"caqT{i}", [128, 4, 128], BF16) for i in range(2)]
                cakT = [sb(st, f"cakT{i}", [128, 4, 128], BF16) for i in range(NRING)]
                caV = [sb(st, f"caV{i}", [128, H, VW], BF16) for i in range(NRING)]
                caP = [sb(st, f"caP{i}", [128, 5, 128], BF16) for i in range(2)]
                caPe = [sb(st, f"caPe{i}", [128, 5, 128], BF16) for i in range(2)]
                kTh = [sb(st, f"kTh{i}", [128, SEQ], BF16) for i in range(2)]
                Vh = [sb(st, f"Vh{i}", [128, NT, VW], BF16) for i in range(2)]
                PT = [sb(st, f"PT{i}", [128, 512], BF16) for i in range(4)]
                og = xb
                oT = xT

                DMA(gpk.t[:, 0:8], g_mix_d[l].rearrange("(k p) -> p k", p=128), [], [gpk], allow_slow_non_contiguous=True)
                DMA(gpk.t[:, 8:16], g_ffn_d[l].rearrange("(k p) -> p k", p=128), [], [gpk], allow_slow_non_contiguous=True)
                DMA(gpk.t[:, 16:20], g_om_d[l].rearrange("(k p) -> p k", p=128), [], [gpk], allow_slow_non_contiguous=True)
                DMA(gpk.t[:, 20:24], g_oc_d[l].rearrange("(k p) -> p k", p=128), [], [gpk], allow_slow_non_contiguous=True)
                DMA(gpk.t[:, 24:26], g_ql_d[l].rearrange("(k p) -> p k", p=128), [], [gpk], allow_slow_non_contiguous=True)
                DMA(gpk.t[:, 26:27], g_kvl_d[l].rearrange("(k p) -> p k", p=128), [], [gpk], allow_slow_non_contiguous=True)
                DMA(gq_b.t[:], g_mq_d[l].partition_broadcast(128), [], [gq_b])
                DMA(gk_b.t[:], g_mk_d[l].partition_broadcast(128), [], [gk_b])
                DMA(gcq_b.t[:], g_cq_d[l].partition_broadcast(128), [], [gcq_b])
                DMA(gck_b.t[:], g_ck_d[l].partition_broadcast(128), [], [gck_b])
                G(lambda: nc.gpsimd.tensor_scalar(out=gq_b.t[:], in0=gq_b.t[:], scalar1=float(QK ** -0.5), scalar2=1.0,
                                                  op0=ALU.mult, op1=ALU.mult), [gq_b], [gq_b])
                G(lambda: nc.gpsimd.tensor_scalar(out=gcq_b.t[:], in0=gcq_b.t[:], scalar1=0.125, scalar2=1.0,
                                                  op0=ALU.mult, op1=ALU.mult), [gcq_b], [gcq_b])

                wcnt = [0]

                def load_w(dst_ap, src_ap, gcol, dst_buf, ncols):
                    i = wcnt[0] % 2
                    wcnt[0] += 1
                    DMA(wst[i].t[:, 0:ncols], src_ap, [], [wst[i]])
                    G(lambda: nc.gpsimd.tensor_scalar(out=dst_ap, in0=wst[i].t[:, 0:ncols], scalar1=gpk.t[:, gcol:gcol + 1],
                                                      scalar2=1.0, op0=ALU.mult, op1=ALU.mult),
                      [wst[i], gpk], [dst_buf])

                for k in range(8):
                    for hlf in range(2):
                        c0 = hlf * 976
                        load_w(Win.t[:, k, c0:c0 + 976], w_in_d[l, k * 128:(k + 1) * 128, c0:c0 + 976], k, Win, 976)
                for k in range(2):
                    load_w(Wuq.t[:, k, :], w_uq_d[l, k * 128:(k + 1) * 128, :], 24 + k, Wuq, 768)
                load_w(Wukv.t[:, :], w_ukv_d[l, :, :], 26, Wukv, 1024)
                for k in range(8):
                    load_w(Wout.t[:, k, :], w_out_d[l, k * 128:(k + 1) * 128, :], 16 + k, Wout, 1024)

                ffn_pieces = []
                for k in range(8):
                    for c0 in range(0, 2 * DFF, 1024):
                        n = min(1024, 2 * DFF - c0)
                        ffn_pieces.append(("up", k, c0, n))
                for c in range(NCH):
                    ffn_pieces.append(("dn", c, 0, 1024))
                ffn_pieces.reverse()

                def convert_piece():
                    if not ffn_pieces:
                        return
                    kind, k, c0, n = ffn_pieces.pop()
                    i = wcnt[0] % 2
                    wcnt[0] += 1
                    if kind == "up":
                        DMA(wst[i].t[:, 0:n], w_up_l[l][k * 128:(k + 1) * 128, c0:c0 + n], [], [wst[i]])
                        G(lambda: nc.gpsimd.tensor_scalar(out=wcb[i].t[:, 0:n], in0=wst[i].t[:, 0:n],
                                                          scalar1=gpk.t[:, 8 + k:9 + k], scalar2=1.0,
                                                          op0=ALU.mult, op1=ALU.mult), [wst[i], gpk], [wcb[i]])
                        DMA(wup_s[k * 128:(k + 1) * 128, c0:c0 + n], wcb[i].t[:, 0:n], [wcb[i]], [r_wup])
                    else:
                        DMA(wst[i].t[:, 0:n], w_down_l[l][k * 128:(k + 1) * 128, :], [], [wst[i]])
                        G(lambda: nc.gpsimd.tensor_copy(out=wcb[i].t[:, 0:n], in_=wst[i].t[:, 0:n]), [wst[i]], [wcb[i]])
                        DMA(wdn_s[k * 128:(k + 1) * 128, :], wcb[i].t[:, 0:n], [wcb[i]], [r_wdn])

                st2 = ExitStack()
                extt = sb(st2, "extt", [H, 768], F32)
                rbt = sb(st2, "rbt", [H, 257], F32)
                hank = sb(st2, "hank", [128, 5, 128], F32)
                DMA(rbt.t[:], relb_d[l], [], [rbt])
                V(lambda: nc.vector.tensor_copy(out=extt.t[:, 0:256], in_=rbt.t[:, 1:257]), [rbt], [extt])
                V(lambda: nc.vector.tensor_copy(out=extt.t[:, 256:768], in_=rbt.t[:, 256:257].broadcast_to([H, 512])),
                  [rbt], [extt])
                DMA(ext_d, extt.t[:], [extt], [r_ext])
                for h in range(H):
                    DMA(hank.t[:], bass.AP(ext_d.tensor, h * 768, [[1, 128], [128, 5], [1, 128]]), [r_ext], [hank])
                    for jj in range(5):
                        bk = banks[0] if jj < 4 else banks[1]
                        col = (jj % 4) * 128
                        PE(lambda: nc.tensor.matmul(bk.t[:, col:col + 128], lhsT=Jf.t[:], rhs=hank.t[:, jj, :], start=True, stop=True),
                           [Jf, hank], [bk])
                    for jj in range(5):
                        bk = banks[0] if jj < 4 else banks[1]
                        col = (jj % 4) * 128
                        A(lambda: nc.scalar.activation(out=Et.t[:, h, 4 - jj, :], in_=bk.t[:, col:col + 128], func=AF.Exp),
                          [bk], [Et])
                    G(lambda: nc.gpsimd.memset(Et.t[0:64, h, 0, 64:128], 0.0), [], [Et])
                    G(lambda: nc.gpsimd.memset(Et.t[64:128, h, 4, 0:64], 0.0), [], [Et])
                S.barrier_all()
                st2.close()
                o_a = sb(st, "o_a", [128, 4, 512], F32)
                o_b = sb(st, "o_b", [128, 4, 512], F32)
                rden = sb(st, "rden", [128, 16], F32)
                G(lambda: nc.gpsimd.memset(qn.t[:, :, 96:128], 0.0), [], [qn])
                G(lambda: nc.gpsimd.memset(kn.t[:, :, 96:128], 0.0), [], [kn])
                for i in range(NRING):
                    G(lambda: nc.gpsimd.memset(caV[i].t[:, :, 64:VW], 1.0), [], [caV[i]])
                for i in range(2):
                    G(lambda: nc.gpsimd.memset(Vh[i].t[:, :, 64:VW], 1.0), [], [Vh[i]])

                if dbg == "s2b":
                    for i in range(DEPTH):
                        DMA(xt[0].t[:, 0:16], w_up_l[i][0:128, 0:16], [], [xt[0]])
                        DMA(xt[0].t[:, 16:32], w_down_l[i][0:128, 0:16], [], [xt[0]])
                    DMA(xt[0].t[0:3, 32:64], conv_w_d[0][:, 0:32], [], [xt[0]])
                    S.barrier_all()
                    return nc
                if dbg == "s2":
                    print("nops at s2", S.nops)
                    S.barrier_all()
                    return nc
                DMA(xt[0].t[:], x_src[0:128, :], [r_xsrc[0]], [xt[0]])

                def prep(t):
                    s4 = t % 4
                    xs = xt[t % 2]
                    if t + 1 < NT:
                        DMA(xt[(t + 1) % 2].t[:], x_src[(t + 1) * 128:(t + 2) * 128, :], [r_xsrc[t + 1]], [xt[(t + 1) % 2]])
                    A(lambda: nc.scalar.activation(out=junk.t[:], in_=xs.t[:], func=AF.Square, scale=1.0 / 32.0,
                                                   accum_out=stat.t[:, 0:1]), [xs], [junk, stat])
                    rstd_of(stat.t[:, 0:1], rs.t[:, 0:1], 1.0, [stat], [rs], stmp)
                    V(lambda: nc.vector.tensor_scalar(out=xb.t[:], in0=xs.t[:], scalar1=rs.t[:, 0:1], scalar2=None, op0=ALU.mult),
                      [xs, rs], [xb])
                    for k in range(8):
                        PE(lambda: nc.tensor.transpose(out=bf(banks[0])[:, k * 128:(k + 1) * 128], in_=xb.t[:, k * 128:(k + 1) * 128],
                                                       identity=ident_b.t[:]), [xb, ident_b], [banks[0]], inc=(k == 7))
                    A(lambda: nc.scalar.copy(out=xT.t[:].rearrange("p k t -> p (k t)"), in_=bf(banks[0])), [banks[0]], [xT])
                    blocks = [(0, 512), (512, 512), (1024, 512), (1536, 416)]
                    for bi, (c0, n) in enumerate(blocks):
                        bk = banks[1 + (bi % 2)]
                        for k in range(8):
                            PE(lambda: nc.tensor.matmul(bk.t[:, 0:n], lhsT=xT.t[:, k, :], rhs=Win.t[:, k, c0:c0 + n],
                                                        start=(k == 0), stop=(k == 7)), [xT, Win], [bk], inc=(k == 7))
                        if bi % 2 == 0:
                            A(lambda: nc.scalar.copy(out=proj.t[:, c0:c0 + n], in_=bk.t[:, 0:n]), [bk], [proj])
                        else:
                            V(lambda: nc.vector.tensor_copy(out=proj.t[:, c0:c0 + n], in_=bk.t[:, 0:n]), [bk], [proj])
                    A(lambda: nc.scalar.activation(out=junk.t[:, 0:256], in_=proj.t[:, 0:256], func=AF.Square, scale=1.0 / 16.0,
                                                   accum_out=stat.t[:, 1:2]), [proj], [junk, stat])
                    A(lambda: nc.scalar.activation(out=junk.t[:, 256:384], in_=proj.t[:, 256:384], func=AF.Square,
                                                   scale=float(128 ** -0.5), accum_out=stat.t[:, 2:3]), [proj], [junk, stat])
                    rstd_of(stat.t[:, 1:3], rs.t[:, 1:3], 1.0, [stat], [rs], stmp)
                    V(lambda: nc.vector.tensor_scalar(out=cb16.t[:, 0:256], in0=proj.t[:, 0:256], scalar1=rs.t[:, 1:2], scalar2=None,
                                                      op0=ALU.mult), [proj, rs], [cb16])
                    V(lambda: nc.vector.tensor_scalar(out=cb16.t[:, 256:384], in0=proj.t[:, 256:384], scalar1=rs.t[:, 2:3], scalar2=None,
                                                      op0=ALU.mult), [proj, rs], [cb16])
                    for k in range(3):
                        PE(lambda: nc.tensor.transpose(out=bf(banks[0])[:, k * 128:(k + 1) * 128], in_=cb16.t[:, k * 128:(k + 1) * 128],
                                                       identity=ident_b.t[:]), [cb16, ident_b], [banks[0]], inc=(k == 2))
                    A(lambda: nc.scalar.copy(out=cT.t[:].rearrange("p k t -> p (k t)"), in_=bf(banks[0])[:, 0:384]), [banks[0]], [cT])
                    for (c0, n, bk) in ((0, 512, banks[1]), (512, 256, banks[2])):
                        for k in range(2):
                            PE(lambda: nc.tensor.matmul(bk.t[:, 0:n], lhsT=cT.t[:, k, :], rhs=Wuq.t[:, k, c0:c0 + n],
                                                        start=(k == 0), stop=(k == 1)), [cT, Wuq], [bk], inc=(k == 1))
                    qf = q_sb.t[:].rearrange("p h d -> p (h d)")
                    A(lambda: nc.scalar.copy(out=qf[:, 0:512], in_=banks[1].t[:, 0:512]), [banks[1]], [q_sb])
                    V(lambda: nc.vector.tensor_copy(out=qf[:, 512:768], in_=banks[2].t[:, 0:256]), [banks[2]], [q_sb])
                    for (c0, bk) in ((0, banks[1]), (512, banks[2])):
                        PE(lambda: nc.tensor.matmul(bk.t[:, 0:512], lhsT=cT.t[:, 2, :], rhs=Wukv.t[:, c0:c0 + 512],
                                                    start=True, stop=True), [cT, Wukv], [bk])
                    kvf = kv_sb.t[:].rearrange("p h d -> p (h d)")
                    A(lambda: nc.scalar.copy(out=kvf[:, 0:512], in_=banks[1].t[:, 0:512]), [banks[1]], [kv_sb])
                    V(lambda: nc.vector.tensor_copy(out=kvf[:, 512:1024], in_=banks[2].t[:, 0:512]), [banks[2]], [kv_sb])
                    sq3 = sq.t[:, 0:768].rearrange("p (h d) -> p h d", d=96)
                    V(lambda: nc.vector.tensor_tensor(out=sq3, in0=q_sb.t[:], in1=q_sb.t[:], op=ALU.mult), [q_sb], [sq])
                    V(lambda: nc.vector.tensor_reduce(out=stat.t[:, 8:16], in_=sq3, axis=AX.X, op=ALU.add), [sq], [stat])
                    sqk = sq.t[:, 0:512].rearrange("p (h d) -> p h d", d=64)
                    V(lambda: nc.vector.tensor_tensor(out=sqk, in0=kv_sb.t[:, :, 0:64], in1=kv_sb.t[:, :, 0:64], op=ALU.mult),
                      [kv_sb], [sq])
                    V(lambda: nc.vector.tensor_reduce(out=stat.t[:, 16:24], in_=sqk, axis=AX.X, op=ALU.add), [sq], [stat])
                    A(lambda: nc.scalar.activation(out=junk.t[:, 0:32], in_=proj.t[:, 384:416], func=AF.Square,
                                                   accum_out=stat.t[:, 3:4]), [proj], [junk, stat])
                    V(lambda: nc.vector.tensor_scalar(out=stat.t[:, 16:24], in0=stat.t[:, 16:24], scalar1=stat.t[:, 3:4], scalar2=None,
                                                      op0=ALU.add), [stat], [stat])
                    caqk = proj.t[:, 416:1440].rearrange("p (h d) -> p h d", d=64)
                    sqc = sq.t[:, 0:1024].rearrange("p (h d) -> p h d", d=64)
                    V(lambda: nc.vector.tensor_tensor(out=sqc, in0=caqk, in1=caqk, op=ALU.mult), [proj], [sq])
                    V(lambda: nc.vector.tensor_reduce(out=stat.t[:, 24:40], in_=sqc, axis=AX.X, op=ALU.add), [sq], [stat])
                    rstd_of(stat.t[:, 8:24], rs.t[:, 8:24], 1.0 / 96.0, [stat], [rs], stmp)
                    rstd_of(stat.t[:, 24:40], rs.t[:, 24:40], 1.0 / 64.0, [stat], [rs], stmp)
                    V(lambda: nc.vector.tensor_tensor(out=q_sb.t[:], in0=q_sb.t[:],
                                                      in1=rs.t[:, 8:16].unsqueeze(2).broadcast_to([128, H, 96]), op=ALU.mult),
                      [q_sb, rs], [q_sb])
                    V(lambda: nc.vector.tensor_tensor(out=qn.t[:, :, 0:64], in0=q_sb.t[:, :, 0:64],
                                                      in1=gq_b.t[:, 0:64].unsqueeze(1).broadcast_to([128, H, 64]), op=ALU.mult),
                      [q_sb, gq_b], [qn])
                    V(lambda: nc.vector.tensor_tensor(out=qrope.t[:], in0=q_sb.t[:, :, 64:96],
                                                      in1=gq_b.t[:, 64:96].unsqueeze(1).broadcast_to([128, H, 32]), op=ALU.mult),
                      [q_sb, gq_b], [qrope])
                    cb_ = cos_t.t[:, t, :].unsqueeze(1).broadcast_to([128, H, 16])
                    sb_ = sin_t.t[:, t, :].unsqueeze(1).broadcast_to([128, H, 16])
                    t1 = qrope.t[:, :, 0:16]
                    t2 = qrope.t[:, :, 16:32]
                    G(lambda: nc.gpsimd.tensor_tensor(out=rtmp.t[:, 0], in0=t1, in1=cb_, op=ALU.mult), [qrope, cos_t], [rtmp])
                    G(lambda: nc.gpsimd.tensor_tensor(out=rtmp.t[:, 1], in0=t2, in1=sb_, op=ALU.mult), [qrope, sin_t], [rtmp])
                    G(lambda: nc.gpsimd.tensor_tensor(out=rtmp.t[:, 2], in0=t1, in1=sb_, op=ALU.mult), [qrope, sin_t], [rtmp])
                    G(lambda: nc.gpsimd.tensor_tensor(out=rtmp.t[:, 3], in0=t2, in1=cb_, op=ALU.mult), [qrope, cos_t], [rtmp])
                    G(lambda: nc.gpsimd.tensor_tensor(out=qn.t[:, :, 64:80], in0=rtmp.t[:, 0], in1=rtmp.t[:, 1], op=ALU.subtract),
                      [rtmp], [qn])
                    G(lambda: nc.gpsimd.tensor_tensor(out=qn.t[:, :, 80:96], in0=rtmp.t[:, 2], in1=rtmp.t[:, 3], op=ALU.add),
                      [rtmp], [qn])
                    V(lambda: nc.vector.tensor_tensor(out=kv_sb.t[:, :, 0:64], in0=kv_sb.t[:, :, 0:64],
                                                      in1=rs.t[:, 16:24].unsqueeze(2).broadcast_to([128, H, 64]), op=ALU.mult),
                      [kv_sb, rs], [kv_sb])
                    V(lambda: nc.vector.tensor_tensor(out=kn.t[:, :, 0:64], in0=kv_sb.t[:, :, 0:64],
                                                      in1=gk_b.t[:, 0:64].unsqueeze(1).broadcast_to([128, H, 64]), op=ALU.mult),
                      [kv_sb, gk_b], [kn])
                    G(lambda: nc.gpsimd.tensor_tensor(out=krg.t[:], in0=proj.t[:, 384:416], in1=gk_b.t[:, 64:96], op=ALU.mult),
                      [proj, gk_b], [krg])
                    c1 = cos_t.t[:, t, :]
                    s1 = sin_t.t[:, t, :]
                    G(lambda: nc.gpsimd.tensor_tensor(out=ktmp.t[:, 0], in0=krg.t[:, 0:16], in1=c1, op=ALU.mult), [krg, cos_t], [ktmp])
                    G(lambda: nc.gpsimd.tensor_tensor(out=ktmp.t[:, 1], in0=krg.t[:, 16:32], in1=s1, op=ALU.mult), [krg, sin_t], [ktmp])
                    G(lambda: nc.gpsimd.tensor_tensor(out=ktmp.t[:, 2], in0=krg.t[:, 0:16], in1=s1, op=ALU.mult), [krg, sin_t], [ktmp])
                    G(lambda: nc.gpsimd.tensor_tensor(out=ktmp.t[:, 3], in0=krg.t[:, 16:32], in1=c1, op=ALU.mult), [krg, cos_t], [ktmp])
                    G(lambda: nc.gpsimd.tensor_tensor(out=krr.t[:, 0:16], in0=ktmp.t[:, 0], in1=ktmp.t[:, 1], op=ALU.subtract),
                      [ktmp], [krr])
                    G(lambda: nc.gpsimd.tensor_tensor(out=krr.t[:, 16:32], in0=ktmp.t[:, 2], in1=ktmp.t[:, 3], op=ALU.add),
                      [ktmp], [krr])
                    V(lambda: nc.vector.tensor_tensor(out=kn.t[:, :, 64:96], in0=krr.t[:, :].unsqueeze(1).broadcast_to([128, H, 32]),
                                                      in1=rs.t[:, 16:24].unsqueeze(2).broadcast_to([128, H, 32]), op=ALU.mult),
                      [krr, rs], [kn])
                    A(lambda: nc.scalar.copy(out=vb.t[:], in_=kv_sb.t[:, :, 64:128]), [kv_sb], [vb])
                    caqk_v = sq.t[:, 0:1024].rearrange("p (h d) -> p h d", d=64)
                    V(lambda: nc.vector.tensor_tensor(out=caqk_v, in0=caqk,
                                                      in1=rs.t[:, 24:40].unsqueeze(2).broadcast_to([128, 16, 64]), op=ALU.mult),
                      [proj, rs], [sq])
                    V(lambda: nc.vector.tensor_tensor(out=caqn.t[:], in0=caqk_v[:, 0:8, :],
                                                      in1=gcq_b.t[:, :].unsqueeze(1).broadcast_to([128, H, 64]), op=ALU.mult),
                      [sq, gcq_b], [caqn])
                    V(lambda: nc.vector.tensor_tensor(out=cakn.t[:], in0=caqk_v[:, 8:16, :],
                                                      in1=gck_b.t[:, :].unsqueeze(1).broadcast_to([128, H, 64]), op=ALU.mult),
                      [sq, gck_b], [cakn])
                    slot = t % NRING
                    A(lambda: nc.scalar.copy(out=caV[slot].t[:, :, 0:64],
                                             in_=proj.t[:, 1440:1952].rearrange("p (h d) -> p h d", d=64)), [proj], [caV[slot]])
                    for h in range(H):
                        PE(lambda: nc.tensor.transpose(out=bf(banks[0])[:, h * 128:(h + 1) * 128], in_=qn.t[:, h, :],
                                                       identity=ident_b.t[:]), [qn, ident_b], [banks[0]], inc=(h == H - 1))
                    A(lambda: nc.scalar.copy(out=qT_st.t[:, :, s4 * 128:(s4 + 1) * 128],
                                             in_=bf(banks[0]).rearrange("p (h t) -> p h t", h=H)), [banks[0]], [qT_st])
                    for h in range(H):
                        PE(lambda: nc.tensor.transpose(out=bf(banks[1])[:, h * 128:(h + 1) * 128], in_=kn.t[:, h, :],
                                                       identity=ident_b.t[:]), [kn, ident_b], [banks[1]], inc=(h == H - 1))
                    ks = kT_stg[t % 2]
                    V(lambda: nc.vector.tensor_copy(out=ks.t[:], in_=bf(banks[1]).rearrange("p (h t) -> p h t", h=H)),
                      [banks[1]], [ks])
                    for h in range(H):
                        DMA(kT_d[h, :, t * 128:(t + 1) * 128], ks.t[:, h, :], [ks], [r_kv[t // 4]])
                        DMA(v_d[h, :, t, :], vb.t[:, h, :], [vb], [r_kv[t // 4]])
                    for pr in range(4):
                        PE(lambda: nc.tensor.transpose(out=bf(banks[2])[:, pr * 128:(pr + 1) * 128],
                                                       in_=caqn.t[:, 2 * pr:2 * pr + 2, :].rearrange("p h d -> p (h d)"),
                                                       identity=ident_b.t[:]), [caqn, ident_b], [banks[2]], inc=(pr == 3))
                    for pr in range(4):
                        PE(lambda: nc.tensor.transpose(out=bf(banks[3])[:, pr * 128:(pr + 1) * 128],
                                                       in_=cakn.t[:, 2 * pr:2 * pr + 2, :].rearrange("p h d -> p (h d)"),
                                                       identity=ident_b.t[:]), [cakn, ident_b], [banks[3]], inc=(pr == 3))
                    caqT = caqT2[t % 2]
                    A(lambda: nc.scalar.copy(out=caqT.t[:].rearrange("p a t -> p (a t)"), in_=bf(banks[2])[:, 0:512]), [banks[2]], [caqT])
                    V(lambda: nc.vector.tensor_copy(out=cakT[slot].t[:].rearrange("p a t -> p (a t)"), in_=bf(banks[3])[:, 0:512]),
                      [banks[3]], [cakT[slot]])
                    convert_piece()
                    convert_piece()
                    convert_piece()

                def chunk_attn(t):
                    s4 = t % 4
                    caqT = caqT2[t % 2]
                    js = [j for j in range(5) if t - 4 + j >= 0]
                    for h in range(H):
                        hp, pr = (h % 2) * 64, h // 2
                        sA = banks[1 + (h % 2) * 2]
                        sB = banks[2 + (h % 2) * 2]
                        cp = caP[h % 2]
                        ce = caPe[h % 2]
                        for j in js:
                            kt = t - 4 + j
                            bk = sA if j < 4 else sB
                            col = (j % 4) * 128
                            PE(lambda: nc.tensor.matmul(bk.t[:, col:col + 128], lhsT=cakT[kt % NRING].t[hp:hp + 64, pr, :],
                                                        rhs=caqT.t[hp:hp + 64, pr, :], start=True, stop=True),
                               [cakT[kt % NRING], caqT], [bk], inc=(j == js[-1] or j == 3))
                        jA = [j for j in js if j < 4]
                        if jA:
                            j0 = jA[0]
                            A(lambda: nc.scalar.activation(out=cp.t[:, j0:4, :].rearrange("p j t -> p (j t)"),
                                                           in_=sA.t[:, j0 * 128:512], func=AF.Exp), [sA], [cp])
                        A(lambda: nc.scalar.activation(out=cp.t[:, 4, :], in_=sB.t[:, 0:128], func=AF.Exp), [sB], [cp])
                        j0 = js[0]
                        V(lambda: nc.vector.tensor_tensor(out=ce.t[:, j0:5, :], in0=cp.t[:, j0:5, :], in1=Et.t[:, h, j0:5, :], op=ALU.mult),
                          [cp, Et], [ce])
                        ob = banks[5 + (h // 4)]
                        for j in js:
                            kt = t - 4 + j
                            PE(lambda: nc.tensor.matmul(ob.t[:, (h % 4) * VW:(h % 4) * VW + VW], lhsT=ce.t[:, j, :],
                                                        rhs=caV[kt % NRING].t[:, h, :], start=(j == js[0]), stop=(j == js[-1])),
                               [ce, caV[kt % NRING]], [ob], inc=(j == js[-1]))
                    for half in range(2):
                        ob = banks[5 + half]
                        o3 = ob.t[:, 0:4 * VW].rearrange("p (h d) -> p h d", d=VW)
                        V(lambda: nc.vector.reciprocal(out=rden.t[:, half * 4:half * 4 + 4], in_=o3[:, :, 64:65].rearrange("p h o -> p (h o)")),
                          [ob], [rden])
                        V(lambda: nc.vector.tensor_tensor(out=o_b.t[:, s4, half * 256:(half + 1) * 256].rearrange("p (h d) -> p h d", d=64),
                                                          in0=o3[:, :, 0:64],
                                                          in1=rden.t[:, half * 4:half * 4 + 4].unsqueeze(2).broadcast_to([128, 4, 64]),
                                                          op=ALU.mult), [ob, rden], [o_b])

                def mla_attn(T):
                    nk = 4 * T + 4
                    kvr = [r_kv[i] for i in range(T + 1)]

                    def load_head(h):
                        DMA(kTh[h % 2].t[:, 0:nk * 128], kT_d[h, :, 0:nk * 128], kvr, [kTh[h % 2]])
                        DMA(Vh[h % 2].t[:, 0:nk, 0:64], v_d[h, :, 0:nk, :], kvr, [Vh[h % 2]])

                    load_head(0)
                    sctr = [0]
                    for h in range(H):
                        if h + 1 < H:
                            load_head(h + 1)
                        kb = kTh[h % 2]
                        vh = Vh[h % 2]
                        pend = None
                        for step in range(nk + 1):
                            if step < nk:
                                kt = step
                                i = kt - 4 * T
                                q0 = max(i, 0) * 128
                                n = 512 - q0
                                sbk = banks[1 + (sctr[0] % 2)]
                                pt = PT[sctr[0] % 4]
                                sctr[0] += 1
                                PE(lambda: nc.tensor.matmul(sbk.t[:, 0:n], lhsT=kb.t[:, kt * 128:(kt + 1) * 128],
                                                            rhs=qT_st.t[:, h, q0:512], start=True, stop=True),
                                   [kb, qT_st], [sbk])
                                if i >= 0:
                                    A(lambda: nc.scalar.activation(out=pt.t[:, q0:q0 + 64], in_=sbk.t[:, 0:64], func=AF.Exp,
                                                                   bias=maskb.t[:, 0:1]), [sbk, maskb], [pt])
                                    A(lambda: nc.scalar.activation(out=pt.t[:, q0 + 64:512], in_=sbk.t[:, 64:n], func=AF.Exp),
                                      [sbk], [pt])
                                else:
                                    A(lambda: nc.scalar.activation(out=pt.t[:, 0:512], in_=sbk.t[:, 0:512], func=AF.Exp), [sbk], [pt])
                                new_pend = (kt, pt, q0 // 128)
                            else:
                                new_pend = None
                            if pend is not None:
                                pkt, ppt, qs0 = pend
                                for qs in range(qs0, 4):
                                    ob = banks[4 + qs]
                                    last = 4 * T + qs
                                    PE(lambda: nc.tensor.matmul(ob.t[:, 0:VW], lhsT=ppt.t[:, qs * 128:(qs + 1) * 128],
                                                                rhs=vh.t[:, pkt, :], start=(pkt == 0), stop=(pkt == last)),
                                       [ppt, vh], [ob], inc=(pkt == last or qs == 3))
                            pend = new_pend
                        for qs in range(4):
                            ob = banks[4 + qs]
                            V(lambda: nc.vector.reciprocal(out=rden.t[:, 8 + qs:9 + qs], in_=ob.t[:, 64:65]), [ob], [rden])
                            V(lambda: nc.vector.tensor_scalar(out=o_a.t[:, qs, h * 64:(h + 1) * 64], in0=ob.t[:, 0:64],
                                                              scalar1=rden.t[:, 8 + qs:9 + qs], scalar2=None, op0=ALU.mult),
                              [ob, rden], [o_a])

                def out_proj(t):
                    s4 = t % 4
                    xres = xr[0]
                    DMA(xres.t[:], x_src[t * 128:(t + 1) * 128, :], [r_xsrc[t]], [xres])
                    A(lambda: nc.scalar.activation(out=junk.t[:, 0:512], in_=o_a.t[:, s4, :], func=AF.Square, scale=float(512 ** -0.5),
                                                   accum_out=stat.t[:, 40:41]), [o_a], [junk, stat])
                    A(lambda: nc.scalar.activation(out=junk.t[:, 512:1024], in_=o_b.t[:, s4, :], func=AF.Square, scale=float(512 ** -0.5),
                                                   accum_out=stat.t[:, 41:42]), [o_b], [junk, stat])
                    rstd_of(stat.t[:, 40:42], rs.t[:, 40:42], 1.0, [stat], [rs], stmp)
                    V(lambda: nc.vector.tensor_scalar(out=og.t[:, 0:512], in0=o_a.t[:, s4, :], scalar1=rs.t[:, 40:41], scalar2=None,
                                                      op0=ALU.mult), [o_a, rs], [og])
                    V(lambda: nc.vector.tensor_scalar(out=og.t[:, 512:1024], in0=o_b.t[:, s4, :], scalar1=rs.t[:, 41:42], scalar2=None,
                                                      op0=ALU.mult), [o_b, rs], [og])
                    for k in range(8):
                        PE(lambda: nc.tensor.transpose(out=bf(banks[0])[:, k * 128:(k + 1) * 128], in_=og.t[:, k * 128:(k + 1) * 128],
                                                       identity=ident_b.t[:]), [og, ident_b], [banks[0]], inc=(k == 7))
                    A(lambda: nc.scalar.copy(out=oT.t[:].rearrange("p k t -> p (k t)"), in_=bf(banks[0])), [banks[0]], [oT])
                    ys = xres
                    for nh in range(2):
                        bk = banks[1 + nh]
                        for k in range(8):
                            PE(lambda: nc.tensor.matmul(bk.t[:, 0:512], lhsT=oT.t[:, k, :], rhs=Wout.t[:, k, nh * 512:(nh + 1) * 512],
                                                        start=(k == 0), stop=(k == 7)), [oT, Wout], [bk], inc=(k == 7))
                        V(lambda: nc.vector.tensor_tensor(out=ys.t[:, nh * 512:(nh + 1) * 512], in0=bk.t[:, 0:512],
                                                          in1=xres.t[:, nh * 512:(nh + 1) * 512], op=ALU.add), [bk, xres], [ys])
                    DMA(y_d[t * 128:(t + 1) * 128, :], ys.t[:], [ys], [r_y[t]])

                for T in range(NST):
                    for s4 in range(4):
                        prep(4 * T + s4)
                        if s4 > 0:
                            chunk_attn(4 * T + s4 - 1)
                    chunk_attn(4 * T + 3)
                    mla_attn(T)
                    if dbg == "s5":
                        S.barrier_all()
                        return nc
                    for s4 in range(4):
                        out_proj(4 * T + s4)
                while ffn_pieces:
                    convert_piece()
                S.barrier_all()

            if dbg == "attn" and l == NL - 1:
                break

            with ExitStack() as st:
                Wdn = sb(st, "Wdn", [128, NCH, D], BF16)
                hT = sb(st, "hT", [128, NCH, 512], BF16)
                xnT = sb(st, "xnT", [128, 8, 512], BF16)
                x4 = [sb(st, f"x4_{i}", [128, D], F32) for i in range(4)]
                xb = sb(st, "xbB", [128, D], BF16)
                junk = sb(st, "junkB", [128, D], BF16)
                stat = sb(st, "statB", [128, 8], F32)
                stmp = sb(st, "stmpB", [128, 8], F32)
                rs = sb(st, "rsB", [128, 8], F32)
                wu = [sb(st, f"wu{i}", [128, 8, 2, 128], BF16) for i in range(3)]
                pre = [sb(st, f"pre{i}", [128, 514], F32) for i in range(4)]
                y0 = [sb(st, f"y0_{i}", [128, 512], F32) for i in range(4)]
                y1 = [sb(st, f"y1_{i}", [128, 512], F32) for i in range(4)]
                sg = [sb(st, f"sg{i}", [128, 512], F32) for i in range(2)]
                halo = sb(st, "halo", [128, 2 * NCH, 2], F32)
                cwr = sb(st, "cwr", [128, 128], F32)
                cwr2 = sb(st, "cwr2", [64, 128], F32)
                cw = sb(st, "cw", [128, 176], F32)
                yo = [sb(st, f"yo{i}", [128, 512], F32) for i in range(2)]

                wdv = wdn_s.rearrange("(c p) n -> p c n", p=128)
                DMA(Wdn.t[:, 0:11, :], wdv[:, 0:11, :], [r_wdn], [Wdn])
                DMA(Wdn.t[:, 11:22, :], wdv[:, 11:22, :], [r_wdn], [Wdn])
                cwv = conv_w_d[l].rearrange("i (c p) -> (i c) p", p=128)
                DMA(cwr.t[:], cwv[0:128, :], [], [cwr])
                G(lambda: nc.gpsimd.memset(cwr2.t[:], 0.0), [], [cwr2])
                DMA(cwr2.t[0:4, :], cwv[128:132, :], [], [cwr2])
                DMA(cwr2.t[4:48, :], conv_b_d[l].rearrange("(c p) -> c p", p=128), [], [cwr2])
                PE(lambda: nc.tensor.transpose(out=banks[0].t[:, 0:128], in_=cwr.t[:], identity=ident_f.t[:]), [cwr, ident_f], [banks[0]])
                PE(lambda: nc.tensor.transpose(out=banks[0].t[:, 128:192], in_=cwr2.t[:], identity=ident_f.t[0:64, 0:64]),
                   [cwr2, ident_f], [banks[0]])
                V(lambda: nc.vector.tensor_copy(out=cw.t[:], in_=banks[0].t[:, 0:176]), [banks[0]], [cw])
                G(lambda: nc.gpsimd.memset(halo.t[:], 0.0), [], [halo])

                def wup_load(c, slot):
                    DMA(wu[slot].t[:, :, 0, :], wup_s[:, c * 128:(c + 1) * 128].rearrange("(k p) n -> p k n", p=128), [r_wup], [wu[slot]])
                    DMA(wu[slot].t[:, :, 1, :], wup_s[:, DFF + c * 128:DFF + (c + 1) * 128].rearrange("(k p) n -> p k n", p=128),
                        [r_wup], [wu[slot]])

                def ffn_norm(m):
                    for s4 in range(4):
                        t = 4 * m + s4
                        xs = x4[s4]
                        DMA(xs.t[:], y_d[t * 128:(t + 1) * 128, :], [r_y[t]], [xs])
                        A(lambda: nc.scalar.activation(out=junk.t[:], in_=xs.t[:], func=AF.Square, scale=1.0 / 32.0,
                                                       accum_out=stat.t[:, s4:s4 + 1]), [xs], [junk, stat])
                        rstd_of(stat.t[:, s4:s4 + 1], rs.t[:, s4:s4 + 1], 1.0, [stat], [rs], stmp)
                        V(lambda: nc.vector.tensor_scalar(out=xb.t[:], in0=xs.t[:], scalar1=rs.t[:, s4:s4 + 1], scalar2=None, op0=ALU.mult),
                          [xs, rs], [xb])
                        for k in range(8):
                            PE(lambda: nc.tensor.transpose(out=bf(banks[0])[:, k * 128:(k + 1) * 128], in_=xb.t[:, k * 128:(k + 1) * 128],
                                                           identity=ident_b.t[:]), [xb, ident_b], [banks[0]], inc=(k == 7))
                        V(lambda: nc.vector.tensor_copy(out=xnT.t[:, :, s4 * 128:(s4 + 1) * 128],
                                                        in_=bf(banks[0]).rearrange("p (k t) -> p k t", k=8)), [banks[0]], [xnT])

                gctr = [0]
                for m in range(NST):
                    ffn_norm(m)
                    wup_load(0, gctr[0] % 3)
                    wup_load(1, (gctr[0] + 1) % 3)
                    for c in range(NCH):
                        ws = wu[gctr[0] % 3]
                        if c + 2 < NCH:
                            wup_load(c + 2, (gctr[0] + 2) % 3)
                        par = gctr[0] % 2
                        gctr[0] += 1
                        res = []
                        for gu in range(2):
                            bk = banks[1 + par * 2 + gu]
                            for k in range(8):
                                PE(lambda: nc.tensor.matmul(bk.t[:, 0:512], lhsT=ws.t[:, k, gu, :], rhs=xnT.t[:, k, :],
                                                            start=(k == 0), stop=(k == 7)), [ws, xnT], [bk], inc=(k == 7))
                            pr_ = pre[par * 2 + gu]
                            a0 = y0[par * 2 + gu]
                            a1 = y1[par * 2 + gu]
                            ch = gu * NCH + c
                            G(lambda: nc.gpsimd.tensor_copy(out=pr_.t[:, 0:2], in_=halo.t[:, ch, :]), [halo], [pr_])
                            A(lambda: nc.scalar.copy(out=pr_.t[:, 2:514], in_=bk.t[:, 0:512]), [bk], [pr_])
                            A(lambda: nc.scalar.activation(out=a0.t[:], in_=bk.t[:, 0:512], func=AF.Identity,
                                                           scale=cw.t[:, 2 * 44 + ch:2 * 44 + ch + 1],
                                                           bias=cw.t[:, 132 + ch:133 + ch]), [bk, cw], [a0])
                            G(lambda: nc.gpsimd.tensor_copy(out=halo.t[:, ch, :], in_=pr_.t[:, 512:514]), [pr_], [halo])
                            V(lambda: nc.vector.scalar_tensor_tensor(out=a1.t[:], in0=pr_.t[:, 1:513], scalar=cw.t[:, 44 + ch:45 + ch],
                                                                     in1=a0.t[:], op0=ALU.mult, op1=ALU.add), [pr_, cw, a0], [a1])
                            V(lambda: nc.vector.scalar_tensor_tensor(out=a0.t[:], in0=pr_.t[:, 0:512], scalar=cw.t[:, ch:ch + 1],
                                                                     in1=a1.t[:], op0=ALU.mult, op1=ALU.add), [pr_, cw, a1], [a0])
                            res.append(a0)
                        sgb = sg[par]
                        A(lambda: nc.scalar.activation(out=sgb.t[:], in_=res[0].t[:], func=AF.Silu), [res[0]], [sgb])
                        V(lambda: nc.vector.tensor_tensor(out=hT.t[:, c, :], in0=sgb.t[:], in1=res[1].t[:], op=ALU.mult),
                          [sgb, res[1]], [hT])
                    for s4 in range(4):
                        t = 4 * m + s4
                        for nh in range(2):
                            bk = banks[5 + nh]
                            for c in range(NCH):
                                PE(lambda: nc.tensor.matmul(bk.t[:, 0:512], lhsT=hT.t[:, c, s4 * 128:(s4 + 1) * 128],
                                                            rhs=Wdn.t[:, c, nh * 512:(nh + 1) * 512], start=(c == 0), stop=(c == NCH - 1)),
                                   [hT, Wdn], [bk], inc=(c == NCH - 1))
                            yb = yo[nh]
                            V(lambda: nc.vector.tensor_tensor(out=yb.t[:], in0=bk.t[:, 0:512], in1=x4[s4].t[:, nh * 512:(nh + 1) * 512],
                                                              op=ALU.add), [bk, x4[s4]], [yb])
                            DMA(y_d[t * 128:(t + 1) * 128, nh * 512:(nh + 1) * 512], yb.t[:], [yb], [r_y[t]])
                S.barrier_all()

        S.finish(r_y)
        S.barrier_all()
        print(f"[build] instrs: " + ", ".join(f"{k}={e.count}" for k, e in S.engs.items()) + f" waits={S.nwait}", flush=True)
    return nc


_NAMES = ["g_mix", "w_in", "w_uq", "w_ukv", "g_q_lora", "g_kv_lora", "g_mla_q", "g_mla_k", "g_ca_q", "g_ca_k",
          "rel_bias", "g_out_mla", "g_out_ca", "w_out", "g_ffn", "w_up", "conv_w", "conv_b", "w_down"]


def kernel(**inputs):
    x = np.ascontiguousarray(inputs["x"], dtype=np.float32)
    pos = np.ascontiguousarray(inputs["positions"], dtype=np.int32)
    B = x.shape[0]
    shared = {n: np.ascontiguousarray(inputs[n], dtype=np.float32) for n in _NAMES if n not in ("w_up", "w_down")}
    for i in range(DEPTH):
        shared[f"w_up_l{i}"] = np.ascontiguousarray(inputs["w_up"][i], dtype=np.float32)
        shared[f"w_down_l{i}"] = np.ascontiguousarray(inputs["w_down"][i], dtype=np.float32)
    nc = build()
    in_maps = []
    for b in range(B):
        m = dict(shared)
        m["x"] = x[b]
        m["positions"] = pos[b]
        in_maps.append(m)
    res = run_bass_kernel_spmd(nc, in_maps, core_ids=list(range(B)))
    return np.stack([np.asarray(r["y"]) for r in res.results], axis=0).astype(np.float32)
```
